# Optimizing a Trainium2 kernel written in Bass

```python
import jax, jax.numpy as jnp
from jax import lax
import numpy as np

D_MODEL = 1024
BATCH = 16
SEQ = 256
DEPTH = 2
DEC_BATCH = 8
DEC_SEQ = 2048
PAST_LEN = 256

GRID_W = 64
N_MIXERS = 2
N_CHUNK_LAYERS = (DEPTH + 1) // 2
N_MLA_LAYERS = DEPTH // 2
CHUNK = 128
D_INNER = 2 * D_MODEL
N_GROUPS = 8
GROUP_DIM = D_INNER // N_GROUPS
N_HEADS = 8
NOPE_DIM = 128
ROPE_DIM = 64
V_DIM = 128
Q_RANK = 384
KV_RANK = 256
Q_BLOCK = 128
ROPE_BASE = 10000.0
D_FF = 2816
CONV_W = 3
EPS = 1e-6

kernel_name = "hybrid_chunkmlp_mla_convffn_diffusion_step"


def rms_norm(x, g):
    xf = x.astype(jnp.float32)
    y = xf * lax.rsqrt(jnp.mean(xf * xf, axis=-1, keepdims=True) + EPS)
    return (y * g.astype(jnp.float32)).astype(x.dtype)


def ada_params(cond, w, b):
    m = (jax.nn.silu(cond) @ w + b)[..., None, :]
    return jnp.split(m, 6, axis=-1)


def modulate(h, shift, scale):
    return h * (1 + scale) + shift


def grid_positions(T):
    rows = T // GRID_W
    r = jnp.repeat(jnp.arange(rows, dtype=jnp.float32), GRID_W)
    col = jnp.tile(jnp.arange(GRID_W, dtype=jnp.float32), rows)
    return r, col


def rope_axial(x, r, col):
    axis_dim = ROPE_DIM // 2
    half = axis_dim // 2
    inv = ROPE_BASE ** (-jnp.arange(half, dtype=jnp.float32) / half)
    xf = x.astype(jnp.float32)
    outs = []
    for a, pos in enumerate((r, col)):
        ang = pos[:, None] * inv[None, :]
        cos = jnp.cos(ang)[:, None, :]
        sin = jnp.sin(ang)[:, None, :]
        xa = xf[..., a * axis_dim:(a + 1) * axis_dim]
        x1, x2 = xa[..., :half], xa[..., half:]
        outs.append(x1 * cos - x2 * sin)
        outs.append(x1 * sin + x2 * cos)
    return jnp.concatenate(outs, axis=-1).astype(x.dtype)


def chunk_mlp_mixer(x, w_in, b_in, g_v, w_s, b_s, w_out):
    B, T, _ = x.shape
    h = jax.nn.gelu(x @ w_in + b_in, approximate=False)
    u, v = jnp.split(h, 2, axis=-1)
    v = rms_norm(v, g_v).reshape(B, T // CHUNK, CHUNK, N_GROUPS, GROUP_DIM)
    mixed = jnp.einsum('gpq,bnqgd->bnpgd', w_s, v) + b_s.T[None, None, :, :, None]
    return (u * mixed.reshape(B, T, D_INNER)) @ w_out


def conv_ffn(x, w_up, conv_w, conv_b, w_down):
    h = x @ w_up
    hp = jnp.pad(h, ((0, 0), (1, 1), (0, 0)))
    h = hp[:, :-2] * conv_w[0] + hp[:, 1:-1] * conv_w[1] + hp[:, 2:] * conv_w[2] + conv_b
    g, val = jnp.split(h, 2, axis=-1)
    return (jax.nn.silu(g) * val) @ w_down


def mla_project_kv(x, w_dkv, g_kv):
    kv = x @ w_dkv
    return rms_norm(kv[..., :KV_RANK], g_kv), kv[..., KV_RANK:]


def mla_queries(x, w_dq, g_q, w_uq):
    B, T, _ = x.shape
    q = (rms_norm(x @ w_dq, g_q) @ w_uq).reshape(B, T, N_HEADS, NOPE_DIM + ROPE_DIM)
    return q[..., :NOPE_DIM], q[..., NOPE_DIM:]


def mla_expand(ckv, w_uk, w_uv):
    B, S, _ = ckv.shape
    k = (ckv @ w_uk).reshape(B, S, N_HEADS, NOPE_DIM)
    v = (ckv @ w_uv).reshape(B, S, N_HEADS, V_DIM)
    return k, v


def mla_attend(q_nope, q_rope, k_nope, k_rope, v):
    scale = (NOPE_DIM + ROPE_DIM) ** -0.5
    s = (jnp.einsum('bqhd,bshd->bhqs', q_nope, k_nope)
         + jnp.einsum('bqhr,bsr->bhqs', q_rope, k_rope))
    p = jax.nn.softmax(s.astype(jnp.float32) * scale, axis=-1).astype(v.dtype)
    return jnp.einsum('bhqs,bshv->bqhv', p, v)


def mla_context(x, w_dq, g_q, w_uq, w_dkv, g_kv, w_uk, w_uv, w_o):
    B, T, _ = x.shape
    ckv, k_rope = mla_project_kv(x, w_dkv, g_kv)
    q_nope, q_rope = mla_queries(x, w_dq, g_q, w_uq)
    k, v = mla_expand(ckv, w_uk, w_uv)
    o = mla_attend(q_nope, q_rope, k, k_rope, v)
    return o.reshape(B, T, N_HEADS * V_DIM) @ w_o, ckv, k_rope


def mla_latent(x, ckv_ctx, krope_ctx, w_dq, g_q, w_uq, w_dkv, g_kv, w_uk, w_uv, w_o):
    B, T, _ = x.shape
    r, col = grid_positions(T)
    ckv, k_rope = mla_project_kv(x, w_dkv, g_kv)
    k_rope = rope_axial(k_rope[:, :, None, :], r, col)[:, :, 0, :]
    q_nope, q_rope = mla_queries(x, w_dq, g_q, w_uq)
    q_rope = rope_axial(q_rope, r, col)
    k_all, v_all = mla_expand(jnp.concatenate([ckv, ckv_ctx.astype(ckv.dtype)], axis=1), w_uk, w_uv)
    kr_all = jnp.concatenate([k_rope, krope_ctx.astype(k_rope.dtype)], axis=1)
    nb = T // Q_BLOCK
    qn_b = q_nope.reshape(B, nb, Q_BLOCK, N_HEADS, NOPE_DIM).transpose(1, 0, 2, 3, 4)
    qr_b = q_rope.reshape(B, nb, Q_BLOCK, N_HEADS, ROPE_DIM).transpose(1, 0, 2, 3, 4)
    o = lax.map(lambda qs: mla_attend(qs[0], qs[1], k_all, kr_all, v_all), (qn_b, qr_b))
    o = o.transpose(1, 0, 2, 3, 4).reshape(B, T, N_HEADS * V_DIM)
    return o @ w_o


def trunk(x, cond, p, ctx_ckv=None, ctx_krope=None):
    is_context = ctx_ckv is None
    new_ckv, new_krope = [], []
    for i in range(DEPTH):
        sh1, sc1, g1, sh2, sc2, g2 = ada_params(cond, p['ada_w'][i], p['ada_b'][i])
        h = modulate(rms_norm(x, p['norm_g'][i, 0]), sh1, sc1)
        j = i // N_MIXERS
        if i % N_MIXERS == 0:
            y = chunk_mlp_mixer(h, p['gm_w_in'][j], p['gm_b_in'][j], p['gm_g_v'][j],
                                p['gm_w_s'][j], p['gm_b_s'][j], p['gm_w_out'][j])
        else:
            mla_w = (p['mla_w_dq'][j], p['mla_g_q'][j], p['mla_w_uq'][j], p['mla_w_dkv'][j],
                     p['mla_g_kv'][j], p['mla_w_uk'][j], p['mla_w_uv'][j], p['mla_w_o'][j])
            if is_context:
                y, ckv, kr = mla_context(h, *mla_w)
                new_ckv.append(ckv)
                new_krope.append(kr)
            else:
                y = mla_latent(h, ctx_ckv[:, j], ctx_krope[:, j], *mla_w)
        x = x + g1 * rms_norm(y, p['norm_g'][i, 1])
        h = modulate(rms_norm(x, p['norm_g'][i, 2]), sh2, sc2)
        y = conv_ffn(h, p['ffn_w_up'][i], p['ffn_conv_w'][i], p['ffn_conv_b'][i], p['ffn_w_down'][i])
        x = x + g2 * rms_norm(y, p['norm_g'][i, 3])
    return x, new_ckv, new_krope


def setup_inputs(seed: int = 0) -> dict:
    key = jax.random.key(seed)
    ks = iter(jax.random.split(key, 40))
    f32 = jnp.float32

    def nrm(shape, scale=1.0):
        return jax.random.normal(next(ks), shape, f32) * scale

    D = D_MODEL
    return {
        'x_prompt': nrm((BATCH, SEQ, D)),
        'x_sample': nrm((DEC_BATCH, DEC_SEQ, D)),
        'cache_ckv': nrm((DEC_BATCH, N_MLA_LAYERS, PAST_LEN, KV_RANK)),
        'cache_krope': nrm((DEC_BATCH, N_MLA_LAYERS, PAST_LEN, ROPE_DIM)),
        'c': nrm((DEC_BATCH, D)),
        'c_ctx': nrm((D,)),
        'ada_w': nrm((DEPTH, D, 6 * D), 0.5 * D ** -0.5),
        'ada_b': nrm((DEPTH, 6 * D), 0.02),
        'norm_g': 1.0 + nrm((DEPTH, 4, D), 0.02),
        'gm_w_in': nrm((N_CHUNK_LAYERS, D, 2 * D_INNER), D ** -0.5),
        'gm_b_in': nrm((N_CHUNK_LAYERS, 2 * D_INNER), 0.02),
        'gm_g_v': 1.0 + nrm((N_CHUNK_LAYERS, D_INNER), 0.02),
        'gm_w_s': nrm((N_CHUNK_LAYERS, N_GROUPS, CHUNK, CHUNK), CHUNK ** -0.5),
        'gm_b_s': 1.0 + nrm((N_CHUNK_LAYERS, N_GROUPS, CHUNK), 0.02),
        'gm_w_out': nrm((N_CHUNK_LAYERS, D_INNER, D), D_INNER ** -0.5),
        'mla_w_dq': nrm((N_MLA_LAYERS, D, Q_RANK), D ** -0.5),
        'mla_g_q': 1.0 + nrm((N_MLA_LAYERS, Q_RANK), 0.02),
        'mla_w_uq': nrm((N_MLA_LAYERS, Q_RANK, N_HEADS * (NOPE_DIM + ROPE_DIM)), Q_RANK ** -0.5),
        'mla_w_dkv': nrm((N_MLA_LAYERS, D, KV_RANK + ROPE_DIM), D ** -0.5),
        'mla_g_kv': 1.0 + nrm((N_MLA_LAYERS, KV_RANK), 0.02),
        'mla_w_uk': nrm((N_MLA_LAYERS, KV_RANK, N_HEADS * NOPE_DIM), KV_RANK ** -0.5),
        'mla_w_uv': nrm((N_MLA_LAYERS, KV_RANK, N_HEADS * V_DIM), KV_RANK ** -0.5),
        'mla_w_o': nrm((N_MLA_LAYERS, N_HEADS * V_DIM, D), (N_HEADS * V_DIM) ** -0.5),
        'ffn_w_up': nrm((DEPTH, D, 2 * D_FF), D ** -0.5),
        'ffn_conv_w': nrm((DEPTH, CONV_W, 2 * D_FF), CONV_W ** -0.5),
        'ffn_conv_b': nrm((DEPTH, 2 * D_FF), 0.02),
        'ffn_w_down': nrm((DEPTH, D_FF, D), D_FF ** -0.5),
    }


def reference(x_prompt, x_sample, cache_ckv, cache_krope, c, c_ctx,
              ada_w, ada_b, norm_g,
              gm_w_in, gm_b_in, gm_g_v, gm_w_s, gm_b_s, gm_w_out,
              mla_w_dq, mla_g_q, mla_w_uq, mla_w_dkv, mla_g_kv, mla_w_uk, mla_w_uv, mla_w_o,
              ffn_w_up, ffn_conv_w, ffn_conv_b, ffn_w_down):
    p = dict(ada_w=ada_w, ada_b=ada_b, norm_g=norm_g,
             gm_w_in=gm_w_in, gm_b_in=gm_b_in, gm_g_v=gm_g_v, gm_w_s=gm_w_s, gm_b_s=gm_b_s,
             gm_w_out=gm_w_out,
             mla_w_dq=mla_w_dq, mla_g_q=mla_g_q, mla_w_uq=mla_w_uq, mla_w_dkv=mla_w_dkv,
             mla_g_kv=mla_g_kv, mla_w_uk=mla_w_uk, mla_w_uv=mla_w_uv, mla_w_o=mla_w_o,
             ffn_w_up=ffn_w_up, ffn_conv_w=ffn_conv_w, ffn_conv_b=ffn_conv_b, ffn_w_down=ffn_w_down)
    y_prompt, ckv_list, krope_list = trunk(x_prompt, c_ctx, p)
    new_ckv = jnp.stack(ckv_list, axis=1)
    new_krope = jnp.stack(krope_list, axis=1)
    y_sample, _, _ = trunk(x_sample, c, p, cache_ckv, cache_krope)
    return (y_prompt, y_sample, new_ckv, new_krope)
```

```python
import contextlib
import numpy as np
import concourse.bass as bass
import concourse.mybir as mybir
from concourse.bass_utils import run_bass_kernel_spmd

F32 = mybir.dt.float32
BF16 = mybir.dt.bfloat16
AF = mybir.ActivationFunctionType
ALU = mybir.AluOpType
ENGS = ('pe', 'dve', 'act', 'pool', 'sp')
SZ = {F32: 4, BF16: 2}

D = 1024
SEQ_P = 256
T_S = 2048
PAST = 256
DI = 2048
NG = 8
NH = 8
QR = 384
KVR = 256
ROPE = 64
DFF = 2816
EPS = 1e-6
NCORES = 8


class View:
    __slots__ = ('buf', 'lo', 'hi', 'ap')

    def __init__(self, buf, lo, hi, ap):
        self.buf, self.lo, self.hi, self.ap = buf, lo, hi, ap

    def with_ap(self, ap):
        return View(self.buf, self.lo, self.hi, ap)


class Arena:
    def __init__(self, k, name, nbytes, space):
        self.k, self.name, self.nbytes = k, name, nbytes
        ctx = k.nc.sbuf_tensor(name, [128, nbytes // 4], F32) if space == 'sbuf' else \
            k.nc.psum_tensor(name, [128, nbytes // 4], F32)
        self.t = k.stack.enter_context(ctx)
        full = self.t[:, :]
        self.aps = {F32: full, BF16: full.bitcast(BF16)}
        self.segs = [[0, nbytes, None, {}]]
        self.top = 0
        self.is_psum = (space != 'sbuf')

    def _split(self, x):
        for i, s in enumerate(self.segs):
            if s[0] < x < s[1]:
                self.segs.insert(i + 1, [x, s[1], s[2], dict(s[3])])
                s[1] = x
                return

    def overlapping(self, lo, hi):
        self._split(lo)
        self._split(hi)
        return [s for s in self.segs if s[0] >= lo and s[1] <= hi]

    def merge(self):
        out = []
        for s in self.segs:
            if out and out[-1][2] == s[2] and out[-1][3] == s[3] and out[-1][1] == s[0]:
                out[-1][1] = s[1]
            else:
                out.append(s)
        self.segs = out

    def alloc(self, dtype, n, name=''):
        base = (self.top + 63) // 64 * 64
        self.top = base + n * SZ[dtype]
        assert self.top <= self.nbytes, ('arena overflow', self.name, name, self.top, self.nbytes)
        return Al(self, dtype, base, n)


class Al:
    def __init__(self, arena, dtype, base, n):
        self.arena, self.dtype, self.base, self.n = arena, dtype, base, n
        self.es = SZ[dtype]
        self.e0 = base // self.es

    def __call__(self, lo=0, hi=None, p0=0, p1=128):
        if hi is None:
            hi = self.n
        assert 0 <= lo < hi <= self.n, (lo, hi, self.n)
        ap = self.arena.aps[self.dtype][p0:p1, self.e0 + lo:self.e0 + hi]
        return View(self.arena, self.base + lo * self.es, self.base + hi * self.es, ap)

    def v3(self, lo, n_outer, stride, n_inner, p0=0, p1=128):
        full = self.arena.aps[self.dtype]
        hi = lo + (n_outer - 1) * stride + n_inner
        assert hi <= self.n, (lo, n_outer, stride, n_inner, self.n)
        if stride == n_inner:
            ap = full[p0:p1, self.e0 + lo:self.e0 + hi].rearrange("p (a b) -> p a b", b=stride)
        else:
            ap = full[p0:p1, self.e0 + lo:self.e0 + lo + n_outer * stride]
            ap = ap.rearrange("p (a b) -> p a b", b=stride)[:, :, 0:n_inner]
        return View(self.arena, self.base + lo * self.es, self.base + hi * self.es, ap)


class K:
    def __init__(self, nc, n_dma_sems=32):
        self.nc = nc
        self.stack = contextlib.ExitStack()
        self.prog = {e: [] for e in ENGS}
        self.cnt = {e: 0 for e in ENGS}
        self.sem = {e: self.stack.enter_context(nc.semaphore('s_' + e)) for e in ENGS}
        self.dsem = [self.stack.enter_context(nc.semaphore('d%d' % i)) for i in range(n_dma_sems)]
        self.dcnt = [0] * n_dma_sems
        half = n_dma_sems // 2
        self.dpool = {'pool': list(range(0, half)), 'sp': list(range(half, n_dma_sems)),
                      'act': list(range(half, n_dma_sems))}
        self.dnext = {'pool': 0, 'sp': 0, 'act': 0}
        self.waited = {e: {} for e in ENGS}
        self.out_tokens = []
        self.n_waits = 0
        self.n_ops = 0
        self.marks = []

    def mark(self, name):
        self.marks.append((name, dict(self.cnt)))

    def _semobj(self, key):
        return self.sem[key] if isinstance(key, str) else self.dsem[key]

    def _deps(self, eng, R, W):
        deps = {}

        def need(tok):
            if deps.get(tok[0], 0) < tok[1]:
                deps[tok[0]] = tok[1]

        for v in R:
            for s in v.buf.overlapping(v.lo, v.hi):
                w = s[2]
                if w is not None and not (w[0] == eng and eng == 'pe'):
                    need(w)
        for v in W:
            for s in v.buf.overlapping(v.lo, v.hi):
                w = s[2]
                if w is not None and w[0] != eng:
                    need(w)
                for k_, val in s[3].items():
                    if k_ != eng:
                        need((k_, val))
        return deps

    def _commit(self, tok, R, W):
        for v in R:
            for s in v.buf.overlapping(v.lo, v.hi):
                if s[3].get(tok[0], 0) < tok[1]:
                    s[3][tok[0]] = tok[1]
        for v in W:
            for s in v.buf.overlapping(v.lo, v.hi):
                s[2] = tok
                s[3] = {}
            v.buf.merge()

    def _emit_waits(self, eng, deps):
        wd = self.waited[eng]
        for k_, v in deps.items():
            if wd.get(k_, 0) >= v:
                continue
            wd[k_] = v
            sem = self._semobj(k_)
            self.prog[eng].append(lambda e, sem=sem, v=v: e.wait_ge(sem, v))
            self.n_waits += 1

    @staticmethod
    def _norm(R, W):
        R2, W2 = [], list()
        for v in W:
            if v.buf.is_psum:
                W2.append(View(v.buf, v.lo // 2048 * 2048, (v.hi + 2047) // 2048 * 2048, None))
            else:
                W2.append(v)
        for v in R:
            if v.buf.is_psum:
                W2.append(View(v.buf, v.lo // 2048 * 2048, (v.hi + 2047) // 2048 * 2048, None))
            else:
                R2.append(v)
        return R2, W2

    def op(self, eng, fn, R=(), W=()):
        R, W = self._norm(R, W)
        deps = self._deps(eng, R, W)
        self._emit_waits(eng, deps)
        self.cnt[eng] += 1
        tok = (eng, self.cnt[eng])
        sem = self.sem[eng]
        self.prog[eng].append(lambda e, fn=fn, sem=sem: fn(e).then_inc(sem, 1))
        self._commit(tok, R, W)
        self.n_ops += 1
        return tok

    def dma(self, q, out_ap, in_ap, R=(), W=(), is_output=False, extra=(), **kw):
        pool_ = self.dpool[q]
        i = pool_[self.dnext[q] % len(pool_)]
        self.dnext[q] += 1
        if q == 'act':
            self.dnext['sp'] = self.dnext[q]
        elif q == 'sp':
            self.dnext['act'] = self.dnext[q]
        deps = self._deps(None, R, W)
        for (ek, ev) in extra:
            if deps.get(ek, 0) < ev:
                deps[ek] = ev
        if self.dcnt[i] > 0 and deps.get(i, 0) < self.dcnt[i]:
            deps[i] = self.dcnt[i]
        self._emit_waits(q, deps)
        self.dcnt[i] += 16
        tok = (i, self.dcnt[i])
        sem = self.dsem[i]
        self.prog[q].append(
            lambda e, o=out_ap, a=in_ap, sem=sem, kw=kw: e.dma_start(out=o, in_=a, **kw).then_inc(sem, 16))
        self._commit(tok, R, W)
        if is_output:
            self.out_tokens.append(tok)
        return tok

    def finish(self, eng='sp'):
        deps = {}
        for k_, v in self.out_tokens:
            if deps.get(k_, 0) < v:
                deps[k_] = v
        self._emit_waits(eng, deps)

    def build(self):
        with self.nc.Block() as block:
            for name, deco in (('pe', block.tensor), ('dve', block.vector), ('act', block.scalar),
                               ('pool', block.gpsimd), ('sp', block.sync)):
                prog = self.prog[name]
                if not prog:
                    continue

                def body(e, prog=prog):
                    for f in prog:
                        f(e)
                deco(body)
        self.stack.close()

    def mm(self, out, lhsT, rhs, start=True, stop=True):
        return self.op('pe', lambda e: e.matmul(out.ap, lhsT.ap, rhs.ap, start=start, stop=stop),
                       R=[lhsT, rhs], W=[out])

    def act(self, out, in_, func, bias=None, scale=1.0, accum=None):
        R = [in_]
        kw = {}
        if bias is not None:
            R.append(bias)
            kw['bias'] = bias.ap
        if isinstance(scale, View):
            R.append(scale)
            kw['scale'] = scale.ap
        else:
            kw['scale'] = float(scale)
        W = [out]
        if accum is not None:
            W.append(accum)
            kw['accum_out'] = accum.ap
        return self.op('act', lambda e: e.activation(out=out.ap, in_=in_.ap, func=func, **kw), R=R, W=W)

    def tt(self, out, a, b, op, eng='dve'):
        return self.op(eng, lambda e: e.tensor_tensor(out=out.ap, in0=a.ap, in1=b.ap, op=op), R=[a, b], W=[out])

    def ts(self, out, a, s1, op0, s2=None, op1=None, eng='dve'):
        R = [a]
        s1v = s1.ap if isinstance(s1, View) else float(s1)
        if isinstance(s1, View):
            R.append(s1)
        if op1 is None:
            return self.op(eng, lambda e: e.tensor_scalar(out=out.ap, in0=a.ap, scalar1=s1v, scalar2=None, op0=op0),
                           R=R, W=[out])
        s2v = s2.ap if isinstance(s2, View) else float(s2)
        if isinstance(s2, View):
            R.append(s2)
        return self.op(eng, lambda e: e.tensor_scalar(out=out.ap, in0=a.ap, scalar1=s1v, scalar2=s2v,
                                                      op0=op0, op1=op1), R=R, W=[out])

    def stt(self, out, a, s, b, op0, op1):
        R = [a, b]
        sv = s.ap if isinstance(s, View) else float(s)
        if isinstance(s, View):
            R.append(s)
        return self.op('dve', lambda e: e.scalar_tensor_tensor(out=out.ap, in0=a.ap, scalar=sv, in1=b.ap,
                                                               op0=op0, op1=op1), R=R, W=[out])

    def copy(self, eng, out, in_):
        if eng == 'act':
            return self.op('act', lambda e: e.copy(out=out.ap, in_=in_.ap), R=[in_], W=[out])
        return self.op(eng, lambda e: e.tensor_copy(out=out.ap, in_=in_.ap), R=[in_], W=[out])

    def memset(self, eng, out, val):
        return self.op(eng, lambda e: e.memset(out.ap, val), R=[], W=[out])


def fm(v):
    v = np.asarray(v, np.float32)
    lead = v.shape[:-1]
    n = v.shape[-1] // 128
    a = v.reshape(lead + (n, 128))
    a = np.moveaxis(a, -1, 0)
    return np.ascontiguousarray(a.reshape(128, -1))


def _pmap():
    off = {}
    o = 0
    for name, n in (('ada_b', 2 * 48), ('norm_g', 2 * 4 * 8), ('b_in_u', 16), ('g_q', 3), ('g_kv', 2),
                    ('conv_w', 2 * 3 * 44), ('conv_b', 2 * 44), ('cond', 8 * 2), ('g_v', 16)):
        off[name] = o
        o += n
    return off, o


PM, NPAR = _pmap()


def build_program(stages=('ada', 'm0', 'f0', 'a', 'b', 'c', 'f1'), groups=(0, 1)):
    nc = bass.Bass("TRN2", target_bir_lowering=False)
    TT = 512 + T_S

    def din(name, shape):
        return nc.dram_tensor(name, list(shape), F32, kind="ExternalInput").ap()

    def dout(name, shape):
        return nc.dram_tensor(name, list(shape), F32, kind="ExternalOutput").ap()

    xT = din("xT", (D, TT))
    params = din("params", (128, NPAR))
    bv_bc = din("bv_bc", (128, DI))
    bs_bc = din("bs_bc", (128, NG * 128))
    wsT = din("wsT", (128, NG * 128))
    cosT = din("cosT", (ROPE, T_S))
    sinT = din("sinT", (ROPE, T_S))
    cckvT = din("cckvT", (KVR, PAST))
    ckrT = din("ckrT", (ROPE, PAST))
    ada_w = din("ada_w", (2, D, 6 * D))
    gm_w_in = din("gm_w_in", (D, 2 * DI))
    gm_w_out = din("gm_w_out", (DI, D))
    w_dq = din("w_dq", (D, QR))
    w_uq = din("w_uq", (QR, NH * 192))
    w_dkv = din("w_dkv", (D, KVR + ROPE))
    w_uk = din("w_uk", (KVR, NH * 128))
    w_uv = din("w_uv", (KVR, NH * 128))
    w_o = din("w_o", (NH * 128, D))
    w_up = din("w_up", (2, D, 2 * DFF))
    w_down = din("w_down", (2, DFF, D))
    yT = dout("yT", (D, TT))
    nckvT = dout("nckvT", (KVR, 512))
    nkrT = dout("nkrT", (ROPE, 512))

    k = K(nc)
    SB = Arena(k, "sb", 212000, 'sbuf')
    PS = Arena(k, "ps", 16 * 1024, 'psum')
    banks = [PS.alloc(F32, 512, 'bank%d' % i) for i in range(8)]
    bank_rr = [0]

    bank_set = [[0, 1, 2, 3, 4, 5, 6]]

    def bank():
        bs = bank_set[0]
        b = banks[bs[bank_rr[0] % len(bs)]]
        bank_rr[0] += 1
        return b
    B_ST = banks[7]

    par = SB.alloc(F32, NPAR, 'params')
    modp = SB.alloc(F32, 2 * 48 * 2, 'mod')
    coef = SB.alloc(F32, 2 * 6 * 8 * 2, 'coef')
    ones = SB.alloc(BF16, 128, 'ones')
    epsb = SB.alloc(F32, 1, 'eps')
    condb = SB.alloc(BF16, 16, 'condb')
    WSLOT = 5632
    NW = 4
    wslots = [SB.alloc(BF16, WSLOT, 'w%d' % i) for i in range(NW)]
    w_rr = [0]
    CACHE_ON = [True]
    xall = SB.alloc(F32, 8 * T_S, 'xall')
    PERSIST_TOP = SB.top

    def P(name, i=0, n=1):
        o = PM[name] + i
        return par(o, o + n)

    NCACHE = 48
    wcache = nc.dram_tensor("wcache", [NCACHE, 128, WSLOT], BF16).ap()
    cache_tok = {}

    def wload(parts, q='pool', key=None):
        s = wslots[w_rr[0] % NW]
        w_rr[0] += 1
        nel = max(off + no * ni for (off, no, ni, ap) in parts)
        if key is not None and key in cache_tok:
            ci_, tok = cache_tok[key]
            dv = s(0, nel)
            k.dma('sp', dv.ap, wcache[ci_][:, 0:nel], W=[dv], extra=[tok])
            return s
        for (off, no, ni, ap) in parts:
            dv = s.v3(off, no, ni, ni) if no > 1 else s(off, off + ni)
            k.dma(q, dv.ap, ap, W=[dv])
        if key is not None and CACHE_ON[0]:
            ci_ = len(cache_tok)
            assert ci_ < NCACHE
            dv = s(0, nel)
            tok = k.dma('sp', wcache[ci_][:, 0:nel], dv.ap, R=[dv])
            cache_tok[key] = (ci_, tok)
        return s

    def wpanel(w2d, c0, ncols, kcs, off=0):
        ap = w2d[0:kcs * 128, c0:c0 + ncols].rearrange("(c p) n -> p c n", p=128)
        return (off, kcs, ncols, ap)

    k.dma('sp', par().ap, params, W=[par()])
    k.memset('dve', ones(), 1.0)
    k.memset('dve', epsb(), EPS)
    k.act(condb(), P('cond', 0, 16), AF.Silu)
    for li in (range(2) if 'ada' in stages else ()):
        for q in range(12):
            s = wload([wpanel(ada_w[li], q * 512, 512, 8)])
            for m in range(4):
                c = q * 4 + m
                b = bank()
                for kc in range(8):
                    k.mm(b(0, 2), s(kc * 512 + m * 128, kc * 512 + m * 128 + 128), condb(kc * 2, kc * 2 + 2),
                         start=(kc == 0), stop=(kc == 7))
                o = (li * 48 + c) * 2
                k.ts(modp(o, o + 2), b(0, 2), P('ada_b', li * 48 + c), ALU.add)

    k.mark('ada_done')

    def MOD(li, which, kc, ci):
        o = ((li * 48) + which * 8 + kc) * 2 + ci
        return modp(o, o + 1)

    def COEF(li, w, kc, ci):
        o = ((li * 6 + w) * 8 + kc) * 2 + ci
        return coef(o, o + 1)

    for li in range(2):
        for kc in range(8):
            for ci in range(2):
                ng = lambda j: P('norm_g', (li * 4 + j) * 8 + kc)
                k.ts(COEF(li, 0, kc, ci), MOD(li, 1, kc, ci), 1.0, ALU.add, ng(0), ALU.mult)
                k.copy('dve', COEF(li, 1, kc, ci), MOD(li, 0, kc, ci))
                k.ts(COEF(li, 2, kc, ci), MOD(li, 2, kc, ci), ng(1), ALU.mult)
                k.ts(COEF(li, 3, kc, ci), MOD(li, 4, kc, ci), 1.0, ALU.add, ng(2), ALU.mult)
                k.copy('dve', COEF(li, 4, kc, ci), MOD(li, 3, kc, ci))
                k.ts(COEF(li, 5, kc, ci), MOD(li, 5, kc, ci), ng(3), ALU.mult)

    def rstd_from(stat_ps, n, inv_d, r_out):
        k.act(r_out, stat_ps, AF.Ln, bias=epsb(), scale=inv_d)
        k.act(r_out, r_out, AF.Exp, scale=-0.5)

    def norm_s1(xcols, n, sq8, st=512):
        for kc in range(8):
            k.act(sq8(kc * st, kc * st + n), xcols(kc), AF.Square)

    def norm_s2(xcols, n, li, wA, ci, hdst, sq8, rbuf, tbuf, st=512):
        for kc in range(8):
            k.mm(B_ST(0, n), ones(), sq8(kc * st, kc * st + n), start=(kc == 0), stop=(kc == 7))
        r = rbuf(0, n)
        rstd_from(B_ST(0, n), n, 1.0 / D, r)
        for kc in range(8):
            t = tbuf[kc % 2](0, n)
            k.tt(t, xcols(kc), r, ALU.mult)
            k.act(hdst(kc), t, AF.Identity, bias=COEF(li, wA + 1, kc, ci), scale=COEF(li, wA, kc, ci))

    def norm_mod(xcols, n, li, wA, ci, hdst, sq8, rbuf, tbuf, st=512):
        norm_s1(xcols, n, sq8, st)
        norm_s2(xcols, n, li, wA, ci, hdst, sq8, rbuf, tbuf, st)

    class WStream:
        def __init__(self):
            self.reqs, self.slots, self.loaded = [], {}, 0

        def request(self, parts, key=None):
            self.reqs.append((parts, key))
            return len(self.reqs) - 1

        def get(self, idx, pf=3):
            while self.loaded < min(len(self.reqs), idx + pf + 1):
                pr_, ky_ = self.reqs[self.loaded]
                self.slots[self.loaded] = wload(pr_, key=ky_)
                self.loaded += 1
            return self.slots[idx]

    psall = Al(PS, F32, 0, 4096)

    def proj_norm_res(n_out_panels, panel_fn, rhs_fn, nk, xcols, n, li, wG, ci, ysb, sq, rbuf, tbuf):
        per = 8 // n_out_panels
        pend = []
        for q in range(n_out_panels):
            s, stride = panel_fn(q)
            for m in range(per):
                c = q * per + m
                b = bank()
                for kc in range(nk):
                    k.mm(b(0, n), s(kc * stride + m * 128, kc * stride + m * 128 + 128), rhs_fn(kc),
                         start=(kc == 0), stop=(kc == nk - 1))
                k.copy('act', ysb(c * n, c * n + n), b(0, n))
                s_ = sq[c % 2](0, n)
                k.act(s_, b(0, n), AF.Square)
                if pend:
                    pc, ps_ = pend.pop()
                    k.mm(B_ST(0, n), ones(), ps_, start=(pc == 0), stop=False)
                pend.append((c, s_))
        pc, ps_ = pend.pop()
        k.mm(B_ST(0, n), ones(), ps_, start=False, stop=True)
        r = rbuf(0, n)
        rstd_from(B_ST(0, n), n, 1.0 / D, r)
        for c in range(8):
            t = tbuf[c % 2](0, n)
            k.tt(t, ysb(c * n, c * n + n), r, ALU.mult)
            k.stt(xcols(c), t, COEF(li, wG, c, ci), xcols(c), ALU.mult, ALU.add)

    def run_group(g_off, TG, seqs, ci, is_prompt):
        SB.top = PERSIST_TOP
        ntile = TG // 512

        def xc(kc, t0, n):
            return xall(kc * TG + t0, kc * TG + t0 + n)

        for kc in range(8):
            k.dma('sp', xc(kc, 0, TG).ap, xT[kc * 128:(kc + 1) * 128, g_off:g_off + TG], W=[xc(kc, 0, TG)])
        G_TOP = SB.top

        SB.top = G_TOP
        h2 = [SB.alloc(BF16, 8 * 512, 'h0'), SB.alloc(BF16, 8 * 512, 'h1')]
        sq8 = SB.alloc(BF16, 8 * 512, 'sq8')
        sq = [Al(SB, BF16, sq8.base, 512), Al(SB, BF16, sq8.base + 1024, 512)]
        rbuf = SB.alloc(F32, 512, 'r')
        tbuf = [SB.alloc(F32, 512, 't0'), SB.alloc(F32, 512, 't1')]
        u = SB.alloc(BF16, 16 * 512, 'u')
        vn = SB.alloc(BF16, 4 * DI, 'vn')
        ysb = Al(SB, F32, vn.base, 8 * 512)
        junk = SB.alloc(BF16, DI, 'junk')
        ssq = SB.alloc(F32, 4, 'ssq')
        bvb = SB.alloc(F32, DI, 'bvb')
        bsb = SB.alloc(F32, NG * 128, 'bsb')
        wsb = SB.alloc(BF16, NG * 128, 'wsb')
        mtmp = SB.alloc(F32, 512, 'mtmp')
        mtmp2 = [SB.alloc(F32, 512, 'mtmpa'), SB.alloc(F32, 512, 'mtmpb')]
        k.dma('sp', bvb().ap, bv_bc, W=[bvb()])
        k.dma('sp', bsb().ap, bs_bc, W=[bsb()])
        k.dma('pool', wsb().ap, wsT, W=[wsb()])
        m0_tiles = list(range(ntile)) if 'm0' in stages else []
        ws = WStream()
        widx = {}
        for ti in m0_tiles:
            for q in range(4):
                widx[(ti, 'v', q)] = ws.request([wpanel(gm_w_in, DI + q * 512, 512, 8)], key=('m0v', q))
            for q in range(4):
                widx[(ti, 'u', q)] = ws.request([wpanel(gm_w_in, q * 512, 512, 8)], key=('m0u', q))
            for q in range(4):
                widx[(ti, 'o', q)] = ws.request([wpanel(gm_w_out, q * 256, 256, 16)], key=('m0o', q))

        def m0_prep1(ti):
            norm_s1(lambda kc: xc(kc, ti * 512, 512), 512, sq8)

        def m0_prep2(ti):
            hh = h2[ti % 2]
            norm_s2(lambda kc: xc(kc, ti * 512, 512), 512, 0, 0, ci, lambda kc: hh(kc * 512, kc * 512 + 512),
                    sq8, rbuf, tbuf)
        if m0_tiles:
            m0_prep1(0)
            m0_prep2(0)
        for ti in m0_tiles:
            t0 = ti * 512
            h = h2[ti % 2]
            nxt = ti + 1 < ntile
            if nxt:
                m0_prep1(ti + 1)
            for q in range(4):
                s = ws.get(widx[(ti, 'v', q)])
                for tc in range(4):
                    b = bank()
                    for kc in range(8):
                        k.mm(b(), h(kc * 512 + tc * 128, kc * 512 + tc * 128 + 128), s(kc * 512, kc * 512 + 512),
                             start=(kc == 0), stop=(kc == 7))
                    mt = mtmp2[(q * 4 + tc) % 2]
                    k.tt(mt(), b(), bvb(q * 512, q * 512 + 512), ALU.add)
                    k.act(vn(tc * DI + q * 512, tc * DI + q * 512 + 512), mt(), AF.Gelu)
                if q == 0 and nxt:
                    m0_prep2(ti + 1)
            for tc in range(4):
                vv = vn(tc * DI, tc * DI + DI)
                k.memset('dve', ssq(tc, tc + 1), 0.0)
                k.act(junk(), vv, AF.Square, accum=ssq(tc, tc + 1))
                k.act(ssq(tc, tc + 1), ssq(tc, tc + 1), AF.Ln, bias=epsb(), scale=1.0 / DI)
                k.act(ssq(tc, tc + 1), ssq(tc, tc + 1), AF.Exp, scale=-0.5)
                k.ts(vv, vv, ssq(tc, tc + 1), ALU.mult)
            LAG = 5

            def u_chunk(c):
                q, m = c // 4, c % 4
                s = ws.get(widx[(ti, 'u', q)])
                b = bank()
                for kc in range(8):
                    k.mm(b(), s(kc * 512 + m * 128, kc * 512 + m * 128 + 128), h(kc * 512, kc * 512 + 512),
                         start=(kc == 0), stop=(kc == 7))
                k.act(u(c * 512, c * 512 + 512), b(), AF.Gelu, bias=P('b_in_u', c))

            def mix_chunk(c):
                g = c // 2
                b = bank()
                for tc in range(4):
                    k.mm(b(tc * 128, tc * 128 + 128), vn(tc * DI + c * 128, tc * DI + c * 128 + 128),
                         wsb(g * 128, g * 128 + 128))
                fullf = SB.aps[F32]
                bap = bass.AP(fullf.tensor, bsb.e0 + g * 128, [[fullf.ap[0][0], 128], [0, 4], [1, 128]])
                bv_ = bsb(g * 128, g * 128 + 128).with_ap(bap)
                mt_ = mtmp2[c % 2]
                k.stt(mt_.v3(0, 4, 128, 128), b.v3(0, 4, 128, 128), P('g_v', c), bv_, ALU.mult, ALU.add)
                k.tt(u(c * 512, c * 512 + 512), mt_(), u(c * 512, c * 512 + 512), ALU.mult)
            for c in range(16 + LAG):
                if c < 16:
                    u_chunk(c)
                if c >= LAG:
                    mix_chunk(c - LAG)
            proj_norm_res(4, lambda q: (ws.get(widx[(ti, 'o', q)]), 256),
                          lambda kc: u(kc * 512, kc * 512 + 512), 16,
                          lambda kc: xc(kc, t0, 512), 512, 0, 2, ci, ysb, sq, rbuf, tbuf)

        k.mark('g%d_m0_done' % ci)
        def ffn(li):
            SB.top = G_TOP
            hs = [[SB.alloc(BF16, 8 * 258, 'hs%d%d' % (b_, s_)) for s_ in range(2)] for b_ in range(2)]
            sq16 = [SB.alloc(BF16, 8 * 258, 'sqa'), SB.alloc(BF16, 8 * 258, 'sqb')]
            sq = [Al(SB, BF16, sq16[0].base, 512), Al(SB, BF16, sq16[0].base + 1024, 512)]
            rb = [SB.alloc(F32, 512, 'r0'), SB.alloc(F32, 512, 'r1')]
            rbuf = rb[0]
            tbuf = [SB.alloc(F32, 512, 't0'), SB.alloc(F32, 512, 't1')]
            gated = SB.alloc(BF16, 22 * 512, 'gated')
            ysb = SB.alloc(F32, 8 * 512, 'ysb')
            cg = [SB.alloc(F32, 512, 'cg%d' % i) for i in range(2)]
            cv = [SB.alloc(F32, 512, 'cv%d' % i) for i in range(2)]
            sg = [SB.alloc(F32, 512, 'sg%d' % i) for i in range(2)]
            hal = SB.alloc(BF16, 8 * 8, 'hal')
            pairs = [(0, 1), (2, 3), (4, 5)]
            pr = [0]
            saved = set()
            for ti in range(1, ntile):
                t0 = ti * 512
                ss, sl = [(a, l) for (a, l) in seqs if a <= t0 < a + l][0]
                if t0 - 1 >= ss:
                    saved.add(ti)
                    norm_mod(lambda kc: xc(kc, t0 - 1, 1), 1, li, 3, ci,
                             lambda kc: hal(kc * 8 + ti, kc * 8 + ti + 1), sq16[0], rbuf, tbuf, st=258)

            def geom(ti, s_i):
                ts0 = ti * 512 + s_i * 256
                ss, sl = [(a, l) for (a, l) in seqs if a <= ts0 < a + l][0]
                use_saved = (s_i == 0 and ti in saved)
                lo = ts0 if use_saved else max(ts0 - 1, ss)
                hi = min(ts0 + 257, ss + sl)
                return lo, hi - lo, lo - (ts0 - 1), use_saved

            def f_prep1(ti):
                for s_i in range(2):
                    lo, n, off, us = geom(ti, s_i)
                    norm_s1(lambda kc: xc(kc, lo, n), n, sq16[s_i], st=258)

            def f_prep2(ti):
                for s_i in range(2):
                    lo, n, off, us = geom(ti, s_i)
                    hbuf = hs[ti % 2][s_i]
                    norm_s2(lambda kc: xc(kc, lo, n), n, li, 3, ci,
                            lambda kc: hbuf(kc * 258 + off, kc * 258 + off + n), sq16[s_i], rb[s_i], tbuf, st=258)
                    for kc in range(8):
                        if off == 1 and us:
                            k.copy('dve', hbuf(kc * 258, kc * 258 + 1), hal(kc * 8 + ti, kc * 8 + ti + 1))
                        elif off == 1:
                            k.memset('dve', hbuf(kc * 258, kc * 258 + 1), 0.0)
                        if off + n < 258:
                            k.memset('dve', hbuf(kc * 258 + 257, kc * 258 + 258), 0.0)
            ws = WStream()
            widx = {}
            for ti in range(ntile):
                for q in range(6):
                    npair = 4 if q < 5 else 2
                    widx[(ti, 'g', q)] = ws.request([wpanel(w_up[li], q * 512, npair * 128, 8)], key=('fg', li, q))
                    widx[(ti, 'v', q)] = ws.request([wpanel(w_up[li], DFF + q * 512, npair * 128, 8)], key=('fv', li, q))
                for q in range(4):
                    widx[(ti, 'd', q)] = ws.request([wpanel(w_down[li], q * 256, 256, 22)], key=('fd', li, q))
            f_prep1(0)
            f_prep2(0)
            ui = [0]
            for ti in range(ntile):
                t0 = ti * 512
                nxt = ti + 1 < ntile
                if nxt:
                    f_prep1(ti + 1)
                for q in range(6):
                    npair = 4 if q < 5 else 2
                    sG = ws.get(widx[(ti, 'g', q)], pf=2)
                    sV = ws.get(widx[(ti, 'v', q)], pf=2)
                    st = npair * 128
                    for m in range(npair):
                        j = q * 4 + m
                        u_ = ui[0] % 2
                        ui[0] += 1
                        for (sW, acc, jj, is_g) in ((sG, cg[u_], j, True), (sV, cv[u_], 22 + j, False)):
                            pa = pairs[pr[0] % 3]
                            pr[0] += 1
                            for s_i in range(2):
                                hbuf = hs[ti % 2][s_i]
                                bb = banks[pa[s_i]]
                                for kc in range(8):
                                    k.mm(bb(0, 258), sW(kc * st + m * 128, kc * st + m * 128 + 128),
                                         hbuf(kc * 258, kc * 258 + 258), start=(kc == 0), stop=(kc == 7))
                            pv = lambda o: psall.v3(pa[0] * 512 + o, 2, 512, 256)
                            a3 = acc.v3(0, 2, 256, 256)
                            cw = lambda t_, jj=jj: P('conv_w', (li * 3 + t_) * 44 + jj)
                            k.act(a3, pv(1), AF.Identity, bias=P('conv_b', li * 44 + jj), scale=cw(1))
                            if not is_g:
                                k.act(sg[u_](), cg[u_](), AF.Silu)
                            k.stt(a3, pv(0), cw(0), a3, ALU.mult, ALU.add)
                            k.stt(a3, pv(2), cw(2), a3, ALU.mult, ALU.add)
                        k.tt(gated(j * 512, j * 512 + 512), sg[u_](), cv[u_](), ALU.mult, eng='pool')
                    if q == 0 and nxt:
                        f_prep2(ti + 1)
                proj_norm_res(4, lambda q: (ws.get(widx[(ti, 'd', q)]), 256),
                              lambda kc: gated(kc * 512, kc * 512 + 512), 22,
                              lambda kc: xc(kc, t0, 512), 512, li, 5, ci, ysb, sq, rbuf, tbuf)

        if 'f0' in stages:
            ffn(0)

        k.mark('g%d_f0_done' % ci)
        SB.top = G_TOP
        n_ctx = 0 if is_prompt else PAST
        S_ALL = TG + n_ctx
        ckvT = SB.alloc(BF16, 2 * S_ALL, 'ckvT')
        krT = SB.alloc(BF16, S_ALL, 'krT')
        k.memset('dve', krT(0, S_ALL, 64, 128), 0.0)
        qlat = SB.alloc(BF16, 3 * TG, 'qlat')
        oT = SB.alloc(BF16, 8 * TG, 'oT')
        L1_TOP = SB.top
        h = SB.alloc(BF16, 8 * 512, 'h')
        sq8 = SB.alloc(BF16, 8 * 512, 'sq8')
        sq = [Al(SB, BF16, sq8.base, 512), Al(SB, BF16, sq8.base + 1024, 512)]
        rbuf = SB.alloc(F32, 512, 'r')
        tbuf = [SB.alloc(F32, 512, 't0'), SB.alloc(F32, 512, 't1')]
        raw = SB.alloc(F32, 3 * 512, 'raw')
        outf = SB.alloc(F32, 3 * 512, 'outf') if is_prompt else None
        if not is_prompt:
            csA = [SB.alloc(F32, 512, 'cosA'), SB.alloc(F32, 512, 'sinA')]
            for m in range(2):
                dv = ckvT(m * S_ALL + TG, m * S_ALL + TG + PAST)
                k.dma('pool', dv.ap, cckvT[m * 128:(m + 1) * 128, :], W=[dv])
            dv = krT(TG, TG + PAST, 0, 64)
            k.dma('pool', dv.ap, ckrT, W=[krT(TG, TG + PAST)])
        permcols = []
        for a in range(2):
            for hf in range(2):
                permcols.append((a * 32 + hf * 16, a * 32 + (1 - hf) * 16))
        for ti in (range(ntile) if 'a' in stages else ()):
            t0 = ti * 512
            norm_mod(lambda kc: xc(kc, t0, 512), 512, 1, 0, ci, lambda kc: h(kc * 512, kc * 512 + 512), sq8, rbuf, tbuf)
            parts = [wpanel(w_dkv, 0, 320, 8)]
            if not is_prompt:
                for (dc, sc) in permcols:
                    parts.append(wpanel(w_dkv, KVR + sc, 16, 8, off=8 * 320 + dc))
            s = wslots[w_rr[0] % NW]
            w_rr[0] += 1
            dv = s.v3(0, 8, 320, 320)
            k.dma('pool', dv.ap, parts[0][3], W=[dv])
            if not is_prompt:
                for (dc, sc) in permcols:
                    dv = s.v3(2560 + dc, 8, 64, 16)
                    ap = w_dkv[:, KVR + sc:KVR + sc + 16].rearrange("(c p) n -> p c n", p=128)
                    k.dma('pool', dv.ap, ap, W=[dv])
            for m in range(2):
                b = bank()
                for kc in range(8):
                    k.mm(b(), s(kc * 320 + m * 128, kc * 320 + m * 128 + 128), h(kc * 512, kc * 512 + 512),
                         start=(kc == 0), stop=(kc == 7))
                k.copy('act', raw(m * 512, m * 512 + 512), b())
                k.act(sq[m](), b(), AF.Square)
                k.mm(B_ST(), ones(), sq[m](), start=(m == 0), stop=(m == 1))
            rstd_from(B_ST(), 512, 1.0 / KVR, rbuf())
            for m in range(2):
                k.tt(tbuf[m](), raw(m * 512, m * 512 + 512), rbuf(), ALU.mult)
                dst = ckvT(m * S_ALL + t0, m * S_ALL + t0 + 512)
                if is_prompt:
                    k.ts(outf(m * 512, m * 512 + 512), tbuf[m](), P('g_kv', m), ALU.mult)
                    k.copy('act', dst, outf(m * 512, m * 512 + 512))
                    k.dma('sp', nckvT[m * 128:(m + 1) * 128, t0:t0 + 512], outf(m * 512, m * 512 + 512).ap,
                          R=[outf(m * 512, m * 512 + 512)], is_output=True)
                else:
                    k.ts(dst, tbuf[m](), P('g_kv', m), ALU.mult)
            b = bank()
            for kc in range(8):
                k.mm(b(0, 512, 0, 64), s(kc * 320 + 256, kc * 320 + 320), h(kc * 512, kc * 512 + 512),
                     start=(kc == 0), stop=(kc == 7))
            if is_prompt:
                k.copy('act', outf(1024, 1536, 0, 64), b(0, 512, 0, 64))
                k.copy('dve', krT(t0, t0 + 512, 0, 64), b(0, 512, 0, 64))
                k.dma('sp', nkrT[:, t0:t0 + 512], outf(1024, 1536, 0, 64).ap, R=[outf(1024, 1536)], is_output=True)
            else:
                b2 = bank()
                for kc in range(8):
                    k.mm(b2(0, 512, 0, 64), s(2560 + kc * 64, 2560 + kc * 64 + 64), h(kc * 512, kc * 512 + 512),
                         start=(kc == 0), stop=(kc == 7))
                k.dma('sp', csA[0](0, 512, 0, 64).ap, cosT[:, t0:t0 + 512], W=[csA[0]()])
                k.dma('sp', csA[1](0, 512, 0, 64).ap, sinT[:, t0:t0 + 512], W=[csA[1]()])
                k.tt(tbuf[0](0, 512, 0, 64), b(0, 512, 0, 64), csA[0](0, 512, 0, 64), ALU.mult)
                k.tt(tbuf[1](0, 512, 0, 64), b2(0, 512, 0, 64), csA[1](0, 512, 0, 64), ALU.mult)
                k.tt(krT(t0, t0 + 512, 0, 64), tbuf[0](0, 512, 0, 64), tbuf[1](0, 512, 0, 64), ALU.add)
            s = wload([wpanel(w_dq, 0, 384, 8)])
            for m in range(3):
                b = bank()
                for kc in range(8):
                    k.mm(b(), s(kc * 384 + m * 128, kc * 384 + m * 128 + 128), h(kc * 512, kc * 512 + 512),
                         start=(kc == 0), stop=(kc == 7))
                k.copy('act', raw(m * 512, m * 512 + 512), b())
                k.act(sq[m % 2](), b(), AF.Square)
                k.mm(B_ST(), ones(), sq[m % 2](), start=(m == 0), stop=(m == 2))
            rstd_from(B_ST(), 512, 1.0 / QR, rbuf())
            for m in range(3):
                k.tt(tbuf[m % 2](), raw(m * 512, m * 512 + 512), rbuf(), ALU.mult)
                k.ts(qlat(m * TG + t0, m * TG + t0 + 512), tbuf[m % 2](), P('g_q', m), ALU.mult)

        k.mark('g%d_A_done' % ci)
        SB.top = L1_TOP
        if not is_prompt:
            csB = [[SB.alloc(F32, 512, 'cosB%d' % i), SB.alloc(F32, 512, 'sinB%d' % i)] for i in range(2)]
        cs_rr = [0]
        KhT = [SB.alloc(BF16, S_ALL, 'KhT%d' % i) for i in range(1)]
        Vh = [SB.alloc(BF16, S_ALL, 'Vh%d' % i) for i in range(1)]
        qn = [SB.alloc(BF16, TG, 'qn%d' % i) for i in range(1)]
        qr = [SB.alloc(BF16, TG, 'qr%d' % i) for i in range(1)]
        k.memset('dve', qr[0](0, TG, 64, 128), 0.0)
        PT = [SB.alloc(BF16, 512, 'PT%d' % i) for i in range(4)]
        rec = [SB.alloc(F32, 512, 'rec0'), SB.alloc(F32, 512, 'rec1')]
        acc_pairs = [(banks[4], banks[5]), (banks[6], banks[7])]
        acc_rr = [0]
        bank_set[0] = [0, 1, 2, 3]
        tb2 = [SB.alloc(F32, 512, 'tb0'), SB.alloc(F32, 512, 'tb1')]
        pt_rr = [0]
        scale = float((128 + ROPE) ** -0.5)
        for hd in (range(NH) if 'b' in stages else ()):
            pp = 0
            s = wslots[w_rr[0] % NW]
            w_rr[0] += 1
            for (off, w2, c0, nc_, kcs) in ((0, w_uk, hd * 128, 128, 2), (256, w_uv, hd * 128, 128, 2),
                                            (512, w_uq, hd * 192, 192, 3)):
                dv = s.v3(off, kcs, nc_, nc_)
                k.dma('pool', dv.ap, w2[0:kcs * 128, c0:c0 + nc_].rearrange("(c p) n -> p c n", p=128), W=[dv])
            if not is_prompt:
                for (dc, sc) in permcols:
                    dv = s.v3(1088 + dc, 3, 64, 16)
                    ap = w_uq[:, hd * 192 + 128 + sc:hd * 192 + 128 + sc + 16].rearrange("(c p) n -> p c n", p=128)
                    k.dma('pool', dv.ap, ap, W=[dv])
            for c0 in range(0, S_ALL, 512):
                n = min(512, S_ALL - c0)
                b = bank()
                for m in range(2):
                    k.mm(b(0, n), s(m * 128, m * 128 + 128), ckvT(m * S_ALL + c0, m * S_ALL + c0 + n),
                         start=(m == 0), stop=(m == 1))
                k.copy('dve', KhT[pp](c0, c0 + n), b(0, n))
            nkc = S_ALL // 128
            for c0 in range(0, nkc, 4):
                nn = min(4, nkc - c0)
                b = bank()
                for j in range(nn):
                    kc_ = c0 + j
                    for m in range(2):
                        k.mm(b(j * 128, j * 128 + 128), ckvT(m * S_ALL + kc_ * 128, m * S_ALL + kc_ * 128 + 128),
                             s(256 + m * 128, 256 + m * 128 + 128), start=(m == 0), stop=(m == 1))
                k.copy('act', Vh[pp](c0 * 128, (c0 + nn) * 128), b(0, nn * 128))
            for t0 in range(0, TG, 512):
                b = bank()
                for m in range(3):
                    k.mm(b(), s(512 + m * 192, 512 + m * 192 + 128), qlat(m * TG + t0, m * TG + t0 + 512),
                         start=(m == 0), stop=(m == 2))
                k.copy('dve', qn[pp](t0, t0 + 512), b())
                b = bank()
                for m in range(3):
                    k.mm(b(0, 512, 0, 64), s(512 + m * 192 + 128, 512 + m * 192 + 192),
                         qlat(m * TG + t0, m * TG + t0 + 512), start=(m == 0), stop=(m == 2))
                if is_prompt:
                    k.copy('dve', qr[pp](t0, t0 + 512, 0, 64), b(0, 512, 0, 64))
                else:
                    b2 = bank()
                    for m in range(3):
                        k.mm(b2(0, 512, 0, 64), s(1088 + m * 64, 1088 + m * 64 + 64),
                             qlat(m * TG + t0, m * TG + t0 + 512), start=(m == 0), stop=(m == 2))
                    cs = csB[cs_rr[0] % 2]
                    cs_rr[0] += 1
                    k.dma('sp', cs[0](0, 512, 0, 64).ap, cosT[:, t0:t0 + 512], W=[cs[0]()])
                    k.dma('sp', cs[1](0, 512, 0, 64).ap, sinT[:, t0:t0 + 512], W=[cs[1]()])
                    k.tt(tb2[0](0, 512, 0, 64), b(0, 512, 0, 64), cs[0](0, 512, 0, 64), ALU.mult)
                    k.tt(tb2[1](0, 512, 0, 64), b2(0, 512, 0, 64), cs[1](0, 512, 0, 64), ALU.mult)
                    k.tt(qr[pp](t0, t0 + 512, 0, 64), tb2[0](0, 512, 0, 64), tb2[1](0, 512, 0, 64), ALU.add)
            LOOK = 2
            for (ss, sl) in seqs:
                keys = [(ss + c * 128) for c in range(sl // 128)] + [(TG + c * 128) for c in range(n_ctx // 128)]
                nk_ = len(keys)
                for q0 in range(ss, ss + sl, 512):
                    nq = min(512, ss + sl - q0)
                    BO, BD = acc_pairs[acc_rr[0] % 2]
                    acc_rr[0] += 1
                    sb_ = {}

                    def S(i):
                        b = bank()
                        kb = keys[i]
                        k.mm(b(0, nq), KhT[pp](kb, kb + 128), qn[pp](q0, q0 + nq), start=True, stop=False)
                        k.mm(b(0, nq), krT(kb, kb + 128), qr[pp](q0, q0 + nq), start=False, stop=True)
                        sb_[i] = b
                    for i in range(min(LOOK, nk_)):
                        S(i)
                    for i in range(nk_):
                        if i + LOOK < nk_:
                            S(i + LOOK)
                        kb = keys[i]
                        p_ = PT[pt_rr[0] % 4]
                        pt_rr[0] += 1
                        k.act(p_(0, nq), sb_.pop(i)(0, nq), AF.Exp, scale=scale)
                        k.mm(BO(0, nq), Vh[pp](kb, kb + 128), p_(0, nq), start=(i == 0), stop=(i == nk_ - 1))
                        k.mm(BD(0, nq), ones(), p_(0, nq), start=(i == 0), stop=(i == nk_ - 1))
                    rc = rec[acc_rr[0] % 2]
                    k.act(rc(0, nq), BD(0, nq), AF.Ln)
                    k.act(rc(0, nq), rc(0, nq), AF.Exp, scale=-1.0)
                    k.tt(oT(hd * TG + q0, hd * TG + q0 + nq), BO(0, nq), rc(0, nq), ALU.mult)
        bank_set[0] = [0, 1, 2, 3, 4, 5, 6]

        k.mark('g%d_B_done' % ci)
        SB.top = L1_TOP
        sq = [SB.alloc(BF16, 512, 'sq0'), SB.alloc(BF16, 512, 'sq1')]
        rbuf = SB.alloc(F32, 512, 'r')
        tbuf = [SB.alloc(F32, 512, 't0'), SB.alloc(F32, 512, 't1')]
        ysb = SB.alloc(F32, 8 * 512, 'ysb')
        for ti in (range(ntile) if 'c' in stages else ()):
            t0 = ti * 512
            proj_norm_res(2, lambda q: (wload([wpanel(w_o, q * 512, 512, 8)], key=('wo', q)), 512),
                          lambda kc: oT(kc * TG + t0, kc * TG + t0 + 512), 8,
                          lambda kc: xc(kc, t0, 512), 512, 1, 2, ci, ysb, sq, rbuf, tbuf)
        k.mark('g%d_C_done' % ci)
        if 'f1' in stages:
            ffn(1)
        k.mark('g%d_f1_done' % ci)
        for kc in range(8):
            k.dma('sp', yT[kc * 128:(kc + 1) * 128, g_off:g_off + TG], xc(kc, 0, TG).ap, R=[xc(kc, 0, TG)],
                  is_output=True)

    if 1 in groups:
        run_group(512, T_S, [(0, T_S)], 1, False)
    if 0 in groups:
        run_group(0, 512, [(0, 256), (256, 256)], 0, True)
    k.finish('sp')
    k.build()
    return nc, k


_CACHE = {}


def _rope_tables():
    half = 16
    inv = (10000.0 ** (-np.arange(half, dtype=np.float32) / half)).astype(np.float32)
    t = np.arange(T_S)
    r = (t // 64).astype(np.float32)
    col = (t % 64).astype(np.float32)
    cosT = np.zeros((ROPE, T_S), np.float32)
    sinT = np.zeros((ROPE, T_S), np.float32)
    for a, pos in enumerate((r, col)):
        ang = (pos[None, :] * inv[:, None]).astype(np.float32)
        c, s = np.cos(ang).astype(np.float32), np.sin(ang).astype(np.float32)
        cosT[a * 32:a * 32 + 16] = c
        cosT[a * 32 + 16:a * 32 + 32] = c
        sinT[a * 32:a * 32 + 16] = -s
        sinT[a * 32 + 16:a * 32 + 32] = s
    return cosT, sinT


def kernel(x_prompt, x_sample, cache_ckv, cache_krope, c, c_ctx, ada_w, ada_b, norm_g,
           gm_w_in, gm_b_in, gm_g_v, gm_w_s, gm_b_s, gm_w_out,
           mla_w_dq, mla_g_q, mla_w_uq, mla_w_dkv, mla_g_kv, mla_w_uk, mla_w_uv, mla_w_o,
           ffn_w_up, ffn_conv_w, ffn_conv_b, ffn_w_down):
    f = lambda a: np.ascontiguousarray(np.asarray(a, np.float32))
    if 'nc' not in _CACHE:
        _CACHE['nc'] = build_program()[0]
    nc = _CACHE['nc']
    cosT, sinT = _rope_tables()
    x_prompt, x_sample = f(x_prompt), f(x_sample)
    shared = {
        "bv_bc": f(np.broadcast_to(f(gm_b_in)[0, DI:], (128, DI))),
        "bs_bc": f(np.broadcast_to(f(gm_b_s)[0].reshape(-1), (128, NG * 128))),
        "wsT": f(np.transpose(f(gm_w_s)[0], (2, 0, 1)).reshape(128, NG * 128)),
        "cosT": cosT, "sinT": sinT,
        "ada_w": f(ada_w), "gm_w_in": f(gm_w_in)[0], "gm_w_out": f(gm_w_out)[0],
        "w_dq": f(mla_w_dq)[0], "w_uq": f(mla_w_uq)[0], "w_dkv": f(mla_w_dkv)[0],
        "w_uk": f(mla_w_uk)[0], "w_uv": f(mla_w_uv)[0], "w_o": f(mla_w_o)[0],
        "w_up": f(ffn_w_up), "w_down": f(ffn_w_down),
    }
    base = np.zeros((128, NPAR), np.float32)
    base[:, PM['ada_b']:PM['ada_b'] + 96] = fm(f(ada_b))
    base[:, PM['norm_g']:PM['norm_g'] + 64] = fm(f(norm_g))
    base[:, PM['b_in_u']:PM['b_in_u'] + 16] = fm(f(gm_b_in)[0, :DI])
    base[:, PM['g_q']:PM['g_q'] + 3] = fm(f(mla_g_q)[0])
    base[:, PM['g_v']:PM['g_v'] + 16] = fm(f(gm_g_v)[0])
    base[:, PM['g_kv']:PM['g_kv'] + 2] = fm(f(mla_g_kv)[0])
    base[:, PM['conv_w']:PM['conv_w'] + 264] = fm(f(ffn_conv_w))
    base[:, PM['conv_b']:PM['conv_b'] + 88] = fm(f(ffn_conv_b))
    in_maps = []
    for i in range(NCORES):
        xp = x_prompt[2 * i:2 * i + 2].reshape(512, D)
        xs = x_sample[i]
        xT = np.ascontiguousarray(np.concatenate([xp, xs], axis=0).T)
        p = base.copy()
        cond = np.stack([f(c_ctx), f(c)[i]], axis=-1)
        p[:, PM['cond']:PM['cond'] + 16] = cond.reshape(8, 128, 2).transpose(1, 0, 2).reshape(128, 16)
        m = dict(shared)
        m["xT"] = xT
        m["params"] = p
        m["cckvT"] = np.ascontiguousarray(f(cache_ckv)[i, 0].T)
        m["ckrT"] = np.ascontiguousarray(f(cache_krope)[i, 0].T)
        in_maps.append(m)
    res = run_bass_kernel_spmd(nc, in_maps, core_ids=list(range(NCORES)))
    y_prompt = np.zeros((16, SEQ_P, D), np.float32)
    y_sample = np.zeros((8, T_S, D), np.float32)
    new_ckv = np.zeros((16, 1, SEQ_P, KVR), np.float32)
    new_krope = np.zeros((16, 1, SEQ_P, ROPE), np.float32)
    for i in range(NCORES):
        r = res.results[i]
        y = np.asarray(r["yT"]).T
        y_prompt[2 * i:2 * i + 2] = y[:512].reshape(2, SEQ_P, D)
        y_sample[i] = y[512:]
        new_ckv[2 * i:2 * i + 2, 0] = np.asarray(r["nckvT"]).T.reshape(2, SEQ_P, KVR)
        new_krope[2 * i:2 * i + 2, 0] = np.asarray(r["nkrT"]).T.reshape(2, SEQ_P, ROPE)
    return (y_prompt, y_sample, new_ckv, new_krope)
```

```python
import contextlib
import numpy as np
import concourse.bass as bass
import concourse.mybir as mybir
from concourse.bass_utils import run_bass_kernel_spmd

F32 = mybir.dt.float32
BF16 = mybir.dt.bfloat16
AF = mybir.ActivationFunctionType
ALU = mybir.AluOpType
ENGS = ('pe', 'dve', 'act', 'pool', 'sp')
SZ = {F32: 4, BF16: 2}

D = 1024
SEQ_P = 256
T_S = 2048
PAST = 256
DI = 2048
NG = 8
NH = 8
QR = 384
KVR = 256
ROPE = 64
DFF = 2816
EPS = 1e-6
NCORES = 8


class View:
    __slots__ = ('buf', 'lo', 'hi', 'ap')

    def __init__(self, buf, lo, hi, ap):
        self.buf, self.lo, self.hi, self.ap = buf, lo, hi, ap

    def with_ap(self, ap):
        return View(self.buf, self.lo, self.hi, ap)


class Arena:
    def __init__(self, k, name, nbytes, space):
        self.k, self.name, self.nbytes = k, name, nbytes
        ctx = k.nc.sbuf_tensor(name, [128, nbytes // 4], F32) if space == 'sbuf' else \
            k.nc.psum_tensor(name, [128, nbytes // 4], F32)
        self.t = k.stack.enter_context(ctx)
        full = self.t[:, :]
        self.aps = {F32: full, BF16: full.bitcast(BF16)}
        self.segs = [[0, nbytes, None, {}]]
        self.top = 0
        self.is_psum = (space != 'sbuf')

    def _split(self, x):
        for i, s in enumerate(self.segs):
            if s[0] < x < s[1]:
                self.segs.insert(i + 1, [x, s[1], s[2], dict(s[3])])
                s[1] = x
                return

    def overlapping(self, lo, hi):
        self._split(lo)
        self._split(hi)
        return [s for s in self.segs if s[0] >= lo and s[1] <= hi]

    def merge(self):
        out = []
        for s in self.segs:
            if out and out[-1][2] == s[2] and out[-1][3] == s[3] and out[-1][1] == s[0]:
                out[-1][1] = s[1]
            else:
                out.append(s)
        self.segs = out

    def alloc(self, dtype, n, name=''):
        base = (self.top + 63) // 64 * 64
        self.top = base + n * SZ[dtype]
        assert self.top <= self.nbytes, ('arena overflow', self.name, name, self.top, self.nbytes)
        return Al(self, dtype, base, n)


class Al:
    def __init__(self, arena, dtype, base, n):
        self.arena, self.dtype, self.base, self.n = arena, dtype, base, n
        self.es = SZ[dtype]
        self.e0 = base // self.es

    def __call__(self, lo=0, hi=None, p0=0, p1=128):
        if hi is None:
            hi = self.n
        assert 0 <= lo < hi <= self.n, (lo, hi, self.n)
        ap = self.arena.aps[self.dtype][p0:p1, self.e0 + lo:self.e0 + hi]
        return View(self.arena, self.base + lo * self.es, self.base + hi * self.es, ap)

    def v3(self, lo, n_outer, stride, n_inner, p0=0, p1=128):
        full = self.arena.aps[self.dtype]
        hi = lo + (n_outer - 1) * stride + n_inner
        assert hi <= self.n, (lo, n_outer, stride, n_inner, self.n)
        if stride == n_inner:
            ap = full[p0:p1, self.e0 + lo:self.e0 + hi].rearrange("p (a b) -> p a b", b=stride)
        else:
            ap = full[p0:p1, self.e0 + lo:self.e0 + lo + n_outer * stride]
            ap = ap.rearrange("p (a b) -> p a b", b=stride)[:, :, 0:n_inner]
        return View(self.arena, self.base + lo * self.es, self.base + hi * self.es, ap)


class K:
    def __init__(self, nc, n_dma_sems=32):
        self.nc = nc
        self.stack = contextlib.ExitStack()
        self.prog = {e: [] for e in ENGS}
        self.cnt = {e: 0 for e in ENGS}
        self.sem = {e: self.stack.enter_context(nc.semaphore('s_' + e)) for e in ENGS}
        self.dsem = [self.stack.enter_context(nc.semaphore('d%d' % i)) for i in range(n_dma_sems)]
        self.dcnt = [0] * n_dma_sems
        half = n_dma_sems // 2
        self.dpool = {'pool': list(range(0, half)), 'sp': list(range(half, n_dma_sems)),
                      'act': list(range(half, n_dma_sems))}
        self.dnext = {'pool': 0, 'sp': 0, 'act': 0}
        self.waited = {e: {} for e in ENGS}
        self.out_tokens = []
        self.n_waits = 0
        self.n_ops = 0
        self.marks = []

    def mark(self, name):
        self.marks.append((name, dict(self.cnt)))

    def _semobj(self, key):
        return self.sem[key] if isinstance(key, str) else self.dsem[key]

    def _deps(self, eng, R, W):
        deps = {}

        def need(tok):
            if deps.get(tok[0], 0) < tok[1]:
                deps[tok[0]] = tok[1]

        for v in R:
            for s in v.buf.overlapping(v.lo, v.hi):
                w = s[2]
                if w is not None and not (w[0] == eng and eng == 'pe'):
                    need(w)
        for v in W:
            for s in v.buf.overlapping(v.lo, v.hi):
                w = s[2]
                if w is not None and w[0] != eng:
                    need(w)
                for k_, val in s[3].items():
                    if k_ != eng:
                        need((k_, val))
        return deps

    def _commit(self, tok, R, W):
        for v in R:
            for s in v.buf.overlapping(v.lo, v.hi):
                if s[3].get(tok[0], 0) < tok[1]:
                    s[3][tok[0]] = tok[1]
        for v in W:
            for s in v.buf.overlapping(v.lo, v.hi):
                s[2] = tok
                s[3] = {}
            v.buf.merge()

    def _emit_waits(self, eng, deps):
        wd = self.waited[eng]
        for k_, v in deps.items():
            if wd.get(k_, 0) >= v:
                continue
            wd[k_] = v
            sem = self._semobj(k_)
            self.prog[eng].append(lambda e, sem=sem, v=v: e.wait_ge(sem, v))
            self.n_waits += 1

    @staticmethod
    def _norm(R, W):
        R2, W2 = [], list()
        for v in W:
            if v.buf.is_psum:
                W2.append(View(v.buf, v.lo // 2048 * 2048, (v.hi + 2047) // 2048 * 2048, None))
            else:
                W2.append(v)
        for v in R:
            if v.buf.is_psum:
                W2.append(View(v.buf, v.lo // 2048 * 2048, (v.hi + 2047) // 2048 * 2048, None))
            else:
                R2.append(v)
        return R2, W2

    def op(self, eng, fn, R=(), W=()):
        R, W = self._norm(R, W)
        deps = self._deps(eng, R, W)
        self._emit_waits(eng, deps)
        self.cnt[eng] += 1
        tok = (eng, self.cnt[eng])
        sem = self.sem[eng]
        self.prog[eng].append(lambda e, fn=fn, sem=sem: fn(e).then_inc(sem, 1))
        self._commit(tok, R, W)
        self.n_ops += 1
        return tok

    def dma(self, q, out_ap, in_ap, R=(), W=(), is_output=False, extra=(), **kw):
        pool_ = self.dpool[q]
        i = pool_[self.dnext[q] % len(pool_)]
        self.dnext[q] += 1
        if q == 'act':
            self.dnext['sp'] = self.dnext[q]
        elif q == 'sp':
            self.dnext['act'] = self.dnext[q]
        deps = self._deps(None, R, W)
        for (ek, ev) in extra:
            if deps.get(ek, 0) < ev:
                deps[ek] = ev
        if self.dcnt[i] > 0 and deps.get(i, 0) < self.dcnt[i]:
            deps[i] = self.dcnt[i]
        self._emit_waits(q, deps)
        self.dcnt[i] += 16
        tok = (i, self.dcnt[i])
        sem = self.dsem[i]
        self.prog[q].append(
            lambda e, o=out_ap, a=in_ap, sem=sem, kw=kw: e.dma_start(out=o, in_=a, **kw).then_inc(sem, 16))
        self._commit(tok, R, W)
        if is_output:
            self.out_tokens.append(tok)
        return tok

    def finish(self, eng='sp'):
        deps = {}
        for k_, v in self.out_tokens:
            if deps.get(k_, 0) < v:
                deps[k_] = v
        self._emit_waits(eng, deps)

    def build(self):
        with self.nc.Block() as block:
            for name, deco in (('pe', block.tensor), ('dve', block.vector), ('act', block.scalar),
                               ('pool', block.gpsimd), ('sp', block.sync)):
                prog = self.prog[name]
                if not prog:
                    continue

                def body(e, prog=prog):
                    for f in prog:
                        f(e)
                deco(body)
        self.stack.close()

    def mm(self, out, lhsT, rhs, start=True, stop=True):
        return self.op('pe', lambda e: e.matmul(out.ap, lhsT.ap, rhs.ap, start=start, stop=stop),
                       R=[lhsT, rhs], W=[out])

    def act(self, out, in_, func, bias=None, scale=1.0, accum=None):
        R = [in_]
        kw = {}
        if bias is not None:
            R.append(bias)
            kw['bias'] = bias.ap
        if isinstance(scale, View):
            R.append(scale)
            kw['scale'] = scale.ap
        else:
            kw['scale'] = float(scale)
        W = [out]
        if accum is not None:
            W.append(accum)
            kw['accum_out'] = accum.ap
        return self.op('act', lambda e: e.activation(out=out.ap, in_=in_.ap, func=func, **kw), R=R, W=W)

    def tt(self, out, a, b, op, eng='dve'):
        return self.op(eng, lambda e: e.tensor_tensor(out=out.ap, in0=a.ap, in1=b.ap, op=op), R=[a, b], W=[out])

    def ts(self, out, a, s1, op0, s2=None, op1=None, eng='dve'):
        R = [a]
        s1v = s1.ap if isinstance(s1, View) else float(s1)
        if isinstance(s1, View):
            R.append(s1)
        if op1 is None:
            return self.op(eng, lambda e: e.tensor_scalar(out=out.ap, in0=a.ap, scalar1=s1v, scalar2=None, op0=op0),
                           R=R, W=[out])
        s2v = s2.ap if isinstance(s2, View) else float(s2)
        if isinstance(s2, View):
            R.append(s2)
        return self.op(eng, lambda e: e.tensor_scalar(out=out.ap, in0=a.ap, scalar1=s1v, scalar2=s2v,
                                                      op0=op0, op1=op1), R=R, W=[out])

    def stt(self, out, a, s, b, op0, op1):
        R = [a, b]
        sv = s.ap if isinstance(s, View) else float(s)
        if isinstance(s, View):
            R.append(s)
        return self.op('dve', lambda e: e.scalar_tensor_tensor(out=out.ap, in0=a.ap, scalar=sv, in1=b.ap,
                                                               op0=op0, op1=op1), R=R, W=[out])

    def copy(self, eng, out, in_):
        if eng == 'act':
            return self.op('act', lambda e: e.copy(out=out.ap, in_=in_.ap), R=[in_], W=[out])
        return self.op(eng, lambda e: e.tensor_copy(out=out.ap, in_=in_.ap), R=[in_], W=[out])

    def memset(self, eng, out, val):
        return self.op(eng, lambda e: e.memset(out.ap, val), R=[], W=[out])


def fm(v):
    v = np.asarray(v, np.float32)
    lead = v.shape[:-1]
    n = v.shape[-1] // 128
    a = v.reshape(lead + (n, 128))
    a = np.moveaxis(a, -1, 0)
    return np.ascontiguousarray(a.reshape(128, -1))


def _pmap():
    off = {}
    o = 0
    for name, n in (('ada_b', 2 * 48), ('norm_g', 2 * 4 * 8), ('b_in_u', 16), ('g_q', 3), ('g_kv', 2),
                    ('conv_w', 2 * 3 * 44), ('conv_b', 2 * 44), ('cond', 8 * 2), ('g_v', 16)):
        off[name] = o
        o += n
    return off, o


PM, NPAR = _pmap()


def build_program(stages=('ada', 'm0', 'f0', 'a', 'b', 'c', 'f1'), groups=(0, 1)):
    nc = bass.Bass("TRN2", target_bir_lowering=False)
    TT = 512 + T_S

    def din(name, shape):
        return nc.dram_tensor(name, list(shape), F32, kind="ExternalInput").ap()

    def dout(name, shape):
        return nc.dram_tensor(name, list(shape), F32, kind="ExternalOutput").ap()

    xT = din("xT", (D, TT))
    params = din("params", (128, NPAR))
    bv_bc = din("bv_bc", (128, DI))
    bs_bc = din("bs_bc", (128, NG * 128))
    wsT = din("wsT", (128, NG * 128))
    cosT = din("cosT", (ROPE, T_S))
    sinT = din("sinT", (ROPE, T_S))
    cckvT = din("cckvT", (KVR, PAST))
    ckrT = din("ckrT", (ROPE, PAST))
    ada_w = din("ada_w", (2, D, 6 * D))
    gm_w_in = din("gm_w_in", (D, 2 * DI))
    gm_w_out = din("gm_w_out", (DI, D))
    w_dq = din("w_dq", (D, QR))
    w_uq = din("w_uq", (QR, NH * 192))
    w_dkv = din("w_dkv", (D, KVR + ROPE))
    w_uk = din("w_uk", (KVR, NH * 128))
    w_uv = din("w_uv", (KVR, NH * 128))
    w_o = din("w_o", (NH * 128, D))
    w_up = din("w_up", (2, D, 2 * DFF))
    w_down = din("w_down", (2, DFF, D))
    yT = dout("yT", (D, TT))
    nckvT = dout("nckvT", (KVR, 512))
    nkrT = dout("nkrT", (ROPE, 512))

    k = K(nc)
    SB = Arena(k, "sb", 212000, 'sbuf')
    PS = Arena(k, "ps", 16 * 1024, 'psum')
    banks = [PS.alloc(F32, 512, 'bank%d' % i) for i in range(8)]
    bank_rr = [0]

    bank_set = [[0, 1, 2, 3, 4, 5, 6]]

    def bank():
        bs = bank_set[0]
        b = banks[bs[bank_rr[0] % len(bs)]]
        bank_rr[0] += 1
        return b
    B_ST = banks[7]

    par = SB.alloc(F32, NPAR, 'params')
    modp = SB.alloc(F32, 2 * 48 * 2, 'mod')
    coef = SB.alloc(F32, 2 * 6 * 8 * 2, 'coef')
    ones = SB.alloc(BF16, 128, 'ones')
    epsb = SB.alloc(F32, 1, 'eps')
    condb = SB.alloc(BF16, 16, 'condb')
    WSLOT = 5632
    NW = 4
    wslots = [SB.alloc(BF16, WSLOT, 'w%d' % i) for i in range(NW)]
    w_rr = [0]
    CACHE_ON = [True]
    xall = SB.alloc(F32, 8 * T_S, 'xall')
    PERSIST_TOP = SB.top

    def P(name, i=0, n=1):
        o = PM[name] + i
        return par(o, o + n)

    NCACHE = 48
    wcache = nc.dram_tensor("wcache", [NCACHE, 128, WSLOT], BF16).ap()
    cache_tok = {}

    def wload(parts, q='pool', key=None):
        s = wslots[w_rr[0] % NW]
        w_rr[0] += 1
        nel = max(off + no * ni for (off, no, ni, ap) in parts)
        if key is not None and key in cache_tok:
            ci_, tok = cache_tok[key]
            dv = s(0, nel)
            k.dma('sp', dv.ap, wcache[ci_][:, 0:nel], W=[dv], extra=[tok])
            return s
        for (off, no, ni, ap) in parts:
            dv = s.v3(off, no, ni, ni) if no > 1 else s(off, off + ni)
            k.dma(q, dv.ap, ap, W=[dv])
        if key is not None and CACHE_ON[0]:
            ci_ = len(cache_tok)
            assert ci_ < NCACHE
            dv = s(0, nel)
            tok = k.dma('sp', wcache[ci_][:, 0:nel], dv.ap, R=[dv])
            cache_tok[key] = (ci_, tok)
        return s

    def wpanel(w2d, c0, ncols, kcs, off=0):
        ap = w2d[0:kcs * 128, c0:c0 + ncols].rearrange("(c p) n -> p c n", p=128)
        return (off, kcs, ncols, ap)

    k.dma('sp', par().ap, params, W=[par()])
    k.memset('dve', ones(), 1.0)
    k.memset('dve', epsb(), EPS)
    k.act(condb(), P('cond', 0, 16), AF.Silu)
    def ada_panel(li, q, s):
        for m in range(4):
            c = q * 4 + m
            b = bank()
            for kc in range(8):
                k.mm(b(0, 2), s(kc * 512 + m * 128, kc * 512 + m * 128 + 128), condb(kc * 2, kc * 2 + 2),
                     start=(kc == 0), stop=(kc == 7))
            o = (li * 48 + c) * 2
            k.ts(modp(o, o + 2), b(0, 2), P('ada_b', li * 48 + c), ALU.add)

    if 'ada' in stages:
        for q in range(12):
            ada_panel(0, q, wload([wpanel(ada_w[0], q * 512, 512, 8)]))

    k.mark('ada_done')

    def MOD(li, which, kc, ci):
        o = ((li * 48) + which * 8 + kc) * 2 + ci
        return modp(o, o + 1)

    def COEF(li, w, kc, ci):
        o = ((li * 6 + w) * 8 + kc) * 2 + ci
        return coef(o, o + 1)

    def coef_layer(li):
        for kc in range(8):
            for ci in range(2):
                ng = lambda j: P('norm_g', (li * 4 + j) * 8 + kc)
                k.ts(COEF(li, 0, kc, ci), MOD(li, 1, kc, ci), 1.0, ALU.add, ng(0), ALU.mult)
                k.copy('dve', COEF(li, 1, kc, ci), MOD(li, 0, kc, ci))
                k.ts(COEF(li, 2, kc, ci), MOD(li, 2, kc, ci), ng(1), ALU.mult)
                k.ts(COEF(li, 3, kc, ci), MOD(li, 4, kc, ci), 1.0, ALU.add, ng(2), ALU.mult)
                k.copy('dve', COEF(li, 4, kc, ci), MOD(li, 3, kc, ci))
                k.ts(COEF(li, 5, kc, ci), MOD(li, 5, kc, ci), ng(3), ALU.mult)
    coef_layer(0)
    ada1_pending = [('ada' in stages)]

    def rstd_from(stat_ps, n, inv_d, r_out):
        k.act(r_out, stat_ps, AF.Ln, bias=epsb(), scale=inv_d)
        k.act(r_out, r_out, AF.Exp, scale=-0.5)

    def norm_s1(xcols, n, sq8, st=512):
        for kc in range(8):
            k.act(sq8(kc * st, kc * st + n), xcols(kc), AF.Square)

    def norm_s2(xcols, n, li, wA, ci, hdst, sq8, rbuf, tbuf, st=512):
        for kc in range(8):
            k.mm(B_ST(0, n), ones(), sq8(kc * st, kc * st + n), start=(kc == 0), stop=(kc == 7))
        r = rbuf(0, n)
        rstd_from(B_ST(0, n), n, 1.0 / D, r)
        for kc in range(8):
            t = tbuf[kc % 2](0, n)
            k.tt(t, xcols(kc), r, ALU.mult)
            k.act(hdst(kc), t, AF.Identity, bias=COEF(li, wA + 1, kc, ci), scale=COEF(li, wA, kc, ci))

    def norm_mod(xcols, n, li, wA, ci, hdst, sq8, rbuf, tbuf, st=512):
        norm_s1(xcols, n, sq8, st)
        norm_s2(xcols, n, li, wA, ci, hdst, sq8, rbuf, tbuf, st)

    class WStream:
        def __init__(self):
            self.reqs, self.slots, self.loaded = [], {}, 0

        def request(self, parts, key=None):
            self.reqs.append((parts, key))
            return len(self.reqs) - 1

        def get(self, idx, pf=3):
            while self.loaded < min(len(self.reqs), idx + pf + 1):
                pr_, ky_ = self.reqs[self.loaded]
                self.slots[self.loaded] = wload(pr_, key=ky_)
                self.loaded += 1
            return self.slots[idx]

    psall = Al(PS, F32, 0, 4096)

    def proj_norm_res(n_out_panels, panel_fn, rhs_fn, nk, xcols, n, li, wG, ci, ysb, sq, rbuf, tbuf):
        per = 8 // n_out_panels
        pend = []
        for q in range(n_out_panels):
            s, stride = panel_fn(q)
            for m in range(per):
                c = q * per + m
                b = bank()
                for kc in range(nk):
                    k.mm(b(0, n), s(kc * stride + m * 128, kc * stride + m * 128 + 128), rhs_fn(kc),
                         start=(kc == 0), stop=(kc == nk - 1))
                k.copy('act', ysb(c * n, c * n + n), b(0, n))
                s_ = sq[c % 2](0, n)
                k.act(s_, b(0, n), AF.Square)
                if pend:
                    pc, ps_ = pend.pop()
                    k.mm(B_ST(0, n), ones(), ps_, start=(pc == 0), stop=False)
                pend.append((c, s_))
        pc, ps_ = pend.pop()
        k.mm(B_ST(0, n), ones(), ps_, start=False, stop=True)
        r = rbuf(0, n)
        rstd_from(B_ST(0, n), n, 1.0 / D, r)
        for c in range(8):
            t = tbuf[c % 2](0, n)
            k.tt(t, ysb(c * n, c * n + n), r, ALU.mult)
            k.stt(xcols(c), t, COEF(li, wG, c, ci), xcols(c), ALU.mult, ALU.add)

    def run_group(g_off, TG, seqs, ci, is_prompt):
        SB.top = PERSIST_TOP
        ntile = TG // 512

        def xc(kc, t0, n):
            return xall(kc * TG + t0, kc * TG + t0 + n)

        for kc in range(8):
            k.dma('sp', xc(kc, 0, TG).ap, xT[kc * 128:(kc + 1) * 128, g_off:g_off + TG], W=[xc(kc, 0, TG)])
        G_TOP = SB.top

        SB.top = G_TOP
        h2 = [SB.alloc(BF16, 8 * 512, 'h0'), SB.alloc(BF16, 8 * 512, 'h1')]
        sq8 = SB.alloc(BF16, 8 * 512, 'sq8')
        sq = [Al(SB, BF16, sq8.base, 512), Al(SB, BF16, sq8.base + 1024, 512)]
        rbuf = SB.alloc(F32, 512, 'r')
        tbuf = [SB.alloc(F32, 512, 't0'), SB.alloc(F32, 512, 't1')]
        u = SB.alloc(BF16, 16 * 512, 'u')
        vn = SB.alloc(BF16, 4 * DI, 'vn')
        ysb = Al(SB, F32, vn.base, 8 * 512)
        junk = SB.alloc(BF16, DI, 'junk')
        ssq = SB.alloc(F32, 4, 'ssq')
        bvb = SB.alloc(F32, DI, 'bvb')
        bsb = SB.alloc(F32, NG * 128, 'bsb')
        wsb = SB.alloc(BF16, NG * 128, 'wsb')
        mtmp = SB.alloc(F32, 512, 'mtmp')
        mtmp2 = [SB.alloc(F32, 512, 'mtmpa'), SB.alloc(F32, 512, 'mtmpb')]
        k.dma('sp', bvb().ap, bv_bc, W=[bvb()])
        k.dma('sp', bsb().ap, bs_bc, W=[bsb()])
        k.dma('pool', wsb().ap, wsT, W=[wsb()])
        m0_tiles = list(range(ntile)) if 'm0' in stages else []
        ws = WStream()
        widx = {}
        do_ada1 = ada1_pending[0] and ntile >= 4
        if do_ada1:
            ada1_pending[0] = False
        for ti in m0_tiles:
            for q in range(4):
                widx[(ti, 'v', q)] = ws.request([wpanel(gm_w_in, DI + q * 512, 512, 8)], key=('m0v', q))
                if do_ada1 and ti >= 1:
                    aq = (ti - 1) * 4 + q
                    widx[(ti, 'a', q)] = ws.request([wpanel(ada_w[1], aq * 512, 512, 8)])
            for q in range(4):
                widx[(ti, 'u', q)] = ws.request([wpanel(gm_w_in, q * 512, 512, 8)], key=('m0u', q))
            for q in range(4):
                widx[(ti, 'o', q)] = ws.request([wpanel(gm_w_out, q * 256, 256, 16)], key=('m0o', q))

        def m0_prep1(ti):
            norm_s1(lambda kc: xc(kc, ti * 512, 512), 512, sq8)

        def m0_prep2(ti):
            hh = h2[ti % 2]
            norm_s2(lambda kc: xc(kc, ti * 512, 512), 512, 0, 0, ci, lambda kc: hh(kc * 512, kc * 512 + 512),
                    sq8, rbuf, tbuf)
        if m0_tiles:
            ws.get(0)
            m0_prep1(0)
            m0_prep2(0)
        for ti in m0_tiles:
            t0 = ti * 512
            h = h2[ti % 2]
            nxt = ti + 1 < ntile
            if nxt:
                m0_prep1(ti + 1)
            for q in range(4):
                s = ws.get(widx[(ti, 'v', q)])
                for tc in range(4):
                    b = bank()
                    for kc in range(8):
                        k.mm(b(), h(kc * 512 + tc * 128, kc * 512 + tc * 128 + 128), s(kc * 512, kc * 512 + 512),
                             start=(kc == 0), stop=(kc == 7))
                    mt = mtmp2[(q * 4 + tc) % 2]
                    k.tt(mt(), b(), bvb(q * 512, q * 512 + 512), ALU.add)
                    k.act(vn(tc * DI + q * 512, tc * DI + q * 512 + 512), mt(), AF.Gelu)
                if (ti, 'a', q) in widx:
                    ada_panel(1, (ti - 1) * 4 + q, ws.get(widx[(ti, 'a', q)]))
                if q == 0 and nxt:
                    m0_prep2(ti + 1)
            for tc in range(4):
                vv = vn(tc * DI, tc * DI + DI)
                k.memset('dve', ssq(tc, tc + 1), 0.0)
                k.act(junk(), vv, AF.Square, accum=ssq(tc, tc + 1))
                k.act(ssq(tc, tc + 1), ssq(tc, tc + 1), AF.Ln, bias=epsb(), scale=1.0 / DI)
                k.act(ssq(tc, tc + 1), ssq(tc, tc + 1), AF.Exp, scale=-0.5)
                k.ts(vv, vv, ssq(tc, tc + 1), ALU.mult)
            LAG = 5

            def u_chunk(c):
                q, m = c // 4, c % 4
                s = ws.get(widx[(ti, 'u', q)])
                b = bank()
                for kc in range(8):
                    k.mm(b(), s(kc * 512 + m * 128, kc * 512 + m * 128 + 128), h(kc * 512, kc * 512 + 512),
                         start=(kc == 0), stop=(kc == 7))
                k.act(u(c * 512, c * 512 + 512), b(), AF.Gelu, bias=P('b_in_u', c))

            def mix_chunk(c):
                g = c // 2
                b = bank()
                for tc in range(4):
                    k.mm(b(tc * 128, tc * 128 + 128), vn(tc * DI + c * 128, tc * DI + c * 128 + 128),
                         wsb(g * 128, g * 128 + 128))
                fullf = SB.aps[F32]
                bap = bass.AP(fullf.tensor, bsb.e0 + g * 128, [[fullf.ap[0][0], 128], [0, 4], [1, 128]])
                bv_ = bsb(g * 128, g * 128 + 128).with_ap(bap)
                mt_ = mtmp2[c % 2]
                k.stt(mt_.v3(0, 4, 128, 128), b.v3(0, 4, 128, 128), P('g_v', c), bv_, ALU.mult, ALU.add)
                k.tt(u(c * 512, c * 512 + 512), mt_(), u(c * 512, c * 512 + 512), ALU.mult)
            for c in range(16 + LAG):
                if c < 16:
                    u_chunk(c)
                if c >= LAG:
                    mix_chunk(c - LAG)
            proj_norm_res(4, lambda q: (ws.get(widx[(ti, 'o', q)]), 256),
                          lambda kc: u(kc * 512, kc * 512 + 512), 16,
                          lambda kc: xc(kc, t0, 512), 512, 0, 2, ci, ysb, sq, rbuf, tbuf)

        if do_ada1:
            coef_layer(1)
        elif ada1_pending[0]:
            ada1_pending[0] = False
            for q in range(12):
                ada_panel(1, q, wload([wpanel(ada_w[1], q * 512, 512, 8)]))
            coef_layer(1)
        k.mark('g%d_m0_done' % ci)
        def ffn(li):
            SB.top = G_TOP
            hs = [[SB.alloc(BF16, 8 * 258, 'hs%d%d' % (b_, s_)) for s_ in range(2)] for b_ in range(2)]
            sq16 = [SB.alloc(BF16, 8 * 258, 'sqa'), SB.alloc(BF16, 8 * 258, 'sqb')]
            sq = [Al(SB, BF16, sq16[0].base, 512), Al(SB, BF16, sq16[0].base + 1024, 512)]
            rb = [SB.alloc(F32, 512, 'r0'), SB.alloc(F32, 512, 'r1')]
            rbuf = rb[0]
            tbuf = [SB.alloc(F32, 512, 't0'), SB.alloc(F32, 512, 't1')]
            gated = SB.alloc(BF16, 22 * 512, 'gated')
            ysb = SB.alloc(F32, 8 * 512, 'ysb')
            cg = [SB.alloc(F32, 512, 'cg%d' % i) for i in range(2)]
            cv = [SB.alloc(F32, 512, 'cv%d' % i) for i in range(2)]
            sg = [SB.alloc(F32, 512, 'sg%d' % i) for i in range(2)]
            hal = SB.alloc(BF16, 8 * 8, 'hal')
            pairs = [(0, 1), (2, 3), (4, 5)]
            pr = [0]
            saved = set()
            for ti in range(1, ntile):
                t0 = ti * 512
                ss, sl = [(a, l) for (a, l) in seqs if a <= t0 < a + l][0]
                if t0 - 1 >= ss:
                    saved.add(ti)

            def geom(ti, s_i):
                ts0 = ti * 512 + s_i * 256
                ss, sl = [(a, l) for (a, l) in seqs if a <= ts0 < a + l][0]
                use_saved = (s_i == 0 and ti in saved)
                lo = ts0 if use_saved else max(ts0 - 1, ss)
                hi = min(ts0 + 257, ss + sl)
                return lo, hi - lo, lo - (ts0 - 1), use_saved

            def f_prep1(ti):
                for s_i in range(2):
                    lo, n, off, us = geom(ti, s_i)
                    norm_s1(lambda kc: xc(kc, lo, n), n, sq16[s_i], st=258)

            def f_prep2(ti):
                for s_i in range(2):
                    lo, n, off, us = geom(ti, s_i)
                    hbuf = hs[ti % 2][s_i]
                    norm_s2(lambda kc: xc(kc, lo, n), n, li, 3, ci,
                            lambda kc: hbuf(kc * 258 + off, kc * 258 + off + n), sq16[s_i], rb[s_i], tbuf, st=258)
                    for kc in range(8):
                        if off == 1 and us:
                            prv = hs[(ti - 1) % 2][1]
                            k.copy('dve', hbuf(kc * 258, kc * 258 + 1), prv(kc * 258 + 256, kc * 258 + 257))
                        elif off == 1:
                            k.memset('dve', hbuf(kc * 258, kc * 258 + 1), 0.0)
                        if off + n < 258:
                            k.memset('dve', hbuf(kc * 258 + 257, kc * 258 + 258), 0.0)
            ws = WStream()
            widx = {}
            for ti in range(ntile):
                for q in range(6):
                    npair = 4 if q < 5 else 2
                    widx[(ti, 'g', q)] = ws.request([wpanel(w_up[li], q * 512, npair * 128, 8)], key=('fg', li, q))
                    widx[(ti, 'v', q)] = ws.request([wpanel(w_up[li], DFF + q * 512, npair * 128, 8)], key=('fv', li, q))
                for q in range(4):
                    widx[(ti, 'd', q)] = ws.request([wpanel(w_down[li], q * 256, 256, 22)], key=('fd', li, q))
            ws.get(0, pf=2)
            f_prep1(0)
            f_prep2(0)
            ui = [0]
            for ti in range(ntile):
                t0 = ti * 512
                nxt = ti + 1 < ntile
                if nxt:
                    f_prep1(ti + 1)
                for q in range(6):
                    npair = 4 if q < 5 else 2
                    sG = ws.get(widx[(ti, 'g', q)], pf=2)
                    sV = ws.get(widx[(ti, 'v', q)], pf=2)
                    st = npair * 128
                    for m in range(npair):
                        j = q * 4 + m
                        u_ = ui[0] % 2
                        ui[0] += 1
                        for (sW, acc, jj, is_g) in ((sG, cg[u_], j, True), (sV, cv[u_], 22 + j, False)):
                            pa = pairs[pr[0] % 3]
                            pr[0] += 1
                            for s_i in range(2):
                                hbuf = hs[ti % 2][s_i]
                                bb = banks[pa[s_i]]
                                for kc in range(8):
                                    k.mm(bb(0, 258), sW(kc * st + m * 128, kc * st + m * 128 + 128),
                                         hbuf(kc * 258, kc * 258 + 258), start=(kc == 0), stop=(kc == 7))
                            pv = lambda o: psall.v3(pa[0] * 512 + o, 2, 512, 256)
                            a3 = acc.v3(0, 2, 256, 256)
                            cw = lambda t_, jj=jj: P('conv_w', (li * 3 + t_) * 44 + jj)
                            k.act(a3, pv(1), AF.Identity, bias=P('conv_b', li * 44 + jj), scale=cw(1))
                            if not is_g:
                                k.act(sg[u_](), cg[u_](), AF.Silu)
                            k.stt(a3, pv(0), cw(0), a3, ALU.mult, ALU.add)
                            k.stt(a3, pv(2), cw(2), a3, ALU.mult, ALU.add)
                        k.tt(gated(j * 512, j * 512 + 512), sg[u_](), cv[u_](), ALU.mult, eng='pool')
                    if q == 0 and nxt:
                        f_prep2(ti + 1)
                proj_norm_res(4, lambda q: (ws.get(widx[(ti, 'd', q)]), 256),
                              lambda kc: gated(kc * 512, kc * 512 + 512), 22,
                              lambda kc: xc(kc, t0, 512), 512, li, 5, ci, ysb, sq, rbuf, tbuf)

        if 'f0' in stages:
            ffn(0)

        k.mark('g%d_f0_done' % ci)
        SB.top = G_TOP
        n_ctx = 0 if is_prompt else PAST
        S_ALL = TG + n_ctx
        ckvT = SB.alloc(BF16, 2 * S_ALL, 'ckvT')
        krT = SB.alloc(BF16, S_ALL, 'krT')
        k.memset('dve', krT(0, S_ALL, 64, 128), 0.0)
        qlat = SB.alloc(BF16, 3 * TG, 'qlat')
        oT = SB.alloc(BF16, 8 * TG, 'oT')
        L1_TOP = SB.top
        h = SB.alloc(BF16, 8 * 512, 'h')
        sq8 = SB.alloc(BF16, 8 * 512, 'sq8')
        sq = [Al(SB, BF16, sq8.base, 512), Al(SB, BF16, sq8.base + 1024, 512)]
        rbuf = SB.alloc(F32, 512, 'r')
        tbuf = [SB.alloc(F32, 512, 't0'), SB.alloc(F32, 512, 't1')]
        raw = SB.alloc(F32, 3 * 512, 'raw')
        outf = SB.alloc(F32, 3 * 512, 'outf') if is_prompt else None
        if not is_prompt:
            csA = [SB.alloc(F32, 512, 'cosA'), SB.alloc(F32, 512, 'sinA')]
            for m in range(2):
                dv = ckvT(m * S_ALL + TG, m * S_ALL + TG + PAST)
                k.dma('pool', dv.ap, cckvT[m * 128:(m + 1) * 128, :], W=[dv])
            dv = krT(TG, TG + PAST, 0, 64)
            k.dma('pool', dv.ap, ckrT, W=[krT(TG, TG + PAST)])
        permcols = []
        for a in range(2):
            for hf in range(2):
                permcols.append((a * 32 + hf * 16, a * 32 + (1 - hf) * 16))
        for ti in (range(ntile) if 'a' in stages else ()):
            t0 = ti * 512
            norm_mod(lambda kc: xc(kc, t0, 512), 512, 1, 0, ci, lambda kc: h(kc * 512, kc * 512 + 512), sq8, rbuf, tbuf)
            parts = [wpanel(w_dkv, 0, 320, 8)]
            if not is_prompt:
                for (dc, sc) in permcols:
                    parts.append(wpanel(w_dkv, KVR + sc, 16, 8, off=8 * 320 + dc))
            s = wslots[w_rr[0] % NW]
            w_rr[0] += 1
            dv = s.v3(0, 8, 320, 320)
            k.dma('pool', dv.ap, parts[0][3], W=[dv])
            if not is_prompt:
                for (dc, sc) in permcols:
                    dv = s.v3(2560 + dc, 8, 64, 16)
                    ap = w_dkv[:, KVR + sc:KVR + sc + 16].rearrange("(c p) n -> p c n", p=128)
                    k.dma('pool', dv.ap, ap, W=[dv])
            for m in range(2):
                b = bank()
                for kc in range(8):
                    k.mm(b(), s(kc * 320 + m * 128, kc * 320 + m * 128 + 128), h(kc * 512, kc * 512 + 512),
                         start=(kc == 0), stop=(kc == 7))
                k.copy('act', raw(m * 512, m * 512 + 512), b())
                k.act(sq[m](), b(), AF.Square)
                k.mm(B_ST(), ones(), sq[m](), start=(m == 0), stop=(m == 1))
            rstd_from(B_ST(), 512, 1.0 / KVR, rbuf())
            for m in range(2):
                k.tt(tbuf[m](), raw(m * 512, m * 512 + 512), rbuf(), ALU.mult)
                dst = ckvT(m * S_ALL + t0, m * S_ALL + t0 + 512)
                if is_prompt:
                    k.ts(outf(m * 512, m * 512 + 512), tbuf[m](), P('g_kv', m), ALU.mult)
                    k.copy('act', dst, outf(m * 512, m * 512 + 512))
                    k.dma('sp', nckvT[m * 128:(m + 1) * 128, t0:t0 + 512], outf(m * 512, m * 512 + 512).ap,
                          R=[outf(m * 512, m * 512 + 512)], is_output=True)
                else:
                    k.ts(dst, tbuf[m](), P('g_kv', m), ALU.mult)
            b = bank()
            for kc in range(8):
                k.mm(b(0, 512, 0, 64), s(kc * 320 + 256, kc * 320 + 320), h(kc * 512, kc * 512 + 512),
                     start=(kc == 0), stop=(kc == 7))
            if is_prompt:
                k.copy('act', outf(1024, 1536, 0, 64), b(0, 512, 0, 64))
                k.copy('dve', krT(t0, t0 + 512, 0, 64), b(0, 512, 0, 64))
                k.dma('sp', nkrT[:, t0:t0 + 512], outf(1024, 1536, 0, 64).ap, R=[outf(1024, 1536)], is_output=True)
            else:
                b2 = bank()
                for kc in range(8):
                    k.mm(b2(0, 512, 0, 64), s(2560 + kc * 64, 2560 + kc * 64 + 64), h(kc * 512, kc * 512 + 512),
                         start=(kc == 0), stop=(kc == 7))
                k.dma('sp', csA[0](0, 512, 0, 64).ap, cosT[:, t0:t0 + 512], W=[csA[0]()])
                k.dma('sp', csA[1](0, 512, 0, 64).ap, sinT[:, t0:t0 + 512], W=[csA[1]()])
                k.tt(tbuf[0](0, 512, 0, 64), b(0, 512, 0, 64), csA[0](0, 512, 0, 64), ALU.mult)
                k.tt(tbuf[1](0, 512, 0, 64), b2(0, 512, 0, 64), csA[1](0, 512, 0, 64), ALU.mult)
                k.tt(krT(t0, t0 + 512, 0, 64), tbuf[0](0, 512, 0, 64), tbuf[1](0, 512, 0, 64), ALU.add)
            s = wload([wpanel(w_dq, 0, 384, 8)])
            for m in range(3):
                b = bank()
                for kc in range(8):
                    k.mm(b(), s(kc * 384 + m * 128, kc * 384 + m * 128 + 128), h(kc * 512, kc * 512 + 512),
                         start=(kc == 0), stop=(kc == 7))
                k.copy('act', raw(m * 512, m * 512 + 512), b())
                k.act(sq[m % 2](), b(), AF.Square)
                k.mm(B_ST(), ones(), sq[m % 2](), start=(m == 0), stop=(m == 2))
            rstd_from(B_ST(), 512, 1.0 / QR, rbuf())
            for m in range(3):
                k.tt(tbuf[m % 2](), raw(m * 512, m * 512 + 512), rbuf(), ALU.mult)
                k.ts(qlat(m * TG + t0, m * TG + t0 + 512), tbuf[m % 2](), P('g_q', m), ALU.mult)

        k.mark('g%d_A_done' % ci)
        SB.top = L1_TOP
        if not is_prompt:
            csB = [[SB.alloc(F32, 512, 'cosB%d' % i), SB.alloc(F32, 512, 'sinB%d' % i)] for i in range(2)]
        cs_rr = [0]
        KhT = [SB.alloc(BF16, S_ALL, 'KhT%d' % i) for i in range(1)]
        Vh = [SB.alloc(BF16, S_ALL, 'Vh%d' % i) for i in range(1)]
        qn = [SB.alloc(BF16, TG, 'qn%d' % i) for i in range(1)]
        qr = [SB.alloc(BF16, TG, 'qr%d' % i) for i in range(1)]
        k.memset('dve', qr[0](0, TG, 64, 128), 0.0)
        PT = [SB.alloc(BF16, 512, 'PT%d' % i) for i in range(4)]
        rec = [SB.alloc(F32, 512, 'rec0'), SB.alloc(F32, 512, 'rec1')]
        acc_pairs = [(banks[4], banks[5]), (banks[6], banks[7])]
        acc_rr = [0]
        bank_set[0] = [0, 1, 2, 3]
        tb2 = [SB.alloc(F32, 512, 'tb0'), SB.alloc(F32, 512, 'tb1')]
        pt_rr = [0]
        scale = float((128 + ROPE) ** -0.5)
        for hd in (range(NH) if 'b' in stages else ()):
            pp = 0
            s = wslots[w_rr[0] % NW]
            w_rr[0] += 1
            for (off, w2, c0, nc_, kcs) in ((0, w_uk, hd * 128, 128, 2), (256, w_uv, hd * 128, 128, 2),
                                            (512, w_uq, hd * 192, 192, 3)):
                dv = s.v3(off, kcs, nc_, nc_)
                k.dma('pool', dv.ap, w2[0:kcs * 128, c0:c0 + nc_].rearrange("(c p) n -> p c n", p=128), W=[dv])
            if not is_prompt:
                for (dc, sc) in permcols:
                    dv = s.v3(1088 + dc, 3, 64, 16)
                    ap = w_uq[:, hd * 192 + 128 + sc:hd * 192 + 128 + sc + 16].rearrange("(c p) n -> p c n", p=128)
                    k.dma('pool', dv.ap, ap, W=[dv])
            for c0 in range(0, S_ALL, 512):
                n = min(512, S_ALL - c0)
                b = bank()
                for m in range(2):
                    k.mm(b(0, n), s(m * 128, m * 128 + 128), ckvT(m * S_ALL + c0, m * S_ALL + c0 + n),
                         start=(m == 0), stop=(m == 1))
                k.copy('dve', KhT[pp](c0, c0 + n), b(0, n))
            nkc = S_ALL // 128
            for c0 in range(0, nkc, 4):
                nn = min(4, nkc - c0)
                b = bank()
                for j in range(nn):
                    kc_ = c0 + j
                    for m in range(2):
                        k.mm(b(j * 128, j * 128 + 128), ckvT(m * S_ALL + kc_ * 128, m * S_ALL + kc_ * 128 + 128),
                             s(256 + m * 128, 256 + m * 128 + 128), start=(m == 0), stop=(m == 1))
                k.copy('act', Vh[pp](c0 * 128, (c0 + nn) * 128), b(0, nn * 128))
            for t0 in range(0, TG, 512):
                b = bank()
                for m in range(3):
                    k.mm(b(), s(512 + m * 192, 512 + m * 192 + 128), qlat(m * TG + t0, m * TG + t0 + 512),
                         start=(m == 0), stop=(m == 2))
                k.copy('dve', qn[pp](t0, t0 + 512), b())
                b = bank()
                for m in range(3):
                    k.mm(b(0, 512, 0, 64), s(512 + m * 192 + 128, 512 + m * 192 + 192),
                         qlat(m * TG + t0, m * TG + t0 + 512), start=(m == 0), stop=(m == 2))
                if is_prompt:
                    k.copy('dve', qr[pp](t0, t0 + 512, 0, 64), b(0, 512, 0, 64))
                else:
                    b2 = bank()
                    for m in range(3):
                        k.mm(b2(0, 512, 0, 64), s(1088 + m * 64, 1088 + m * 64 + 64),
                             qlat(m * TG + t0, m * TG + t0 + 512), start=(m == 0), stop=(m == 2))
                    cs = csB[cs_rr[0] % 2]
                    cs_rr[0] += 1
                    k.dma('sp', cs[0](0, 512, 0, 64).ap, cosT[:, t0:t0 + 512], W=[cs[0]()])
                    k.dma('sp', cs[1](0, 512, 0, 64).ap, sinT[:, t0:t0 + 512], W=[cs[1]()])
                    k.tt(tb2[0](0, 512, 0, 64), b(0, 512, 0, 64), cs[0](0, 512, 0, 64), ALU.mult)
                    k.tt(tb2[1](0, 512, 0, 64), b2(0, 512, 0, 64), cs[1](0, 512, 0, 64), ALU.mult)
                    k.tt(qr[pp](t0, t0 + 512, 0, 64), tb2[0](0, 512, 0, 64), tb2[1](0, 512, 0, 64), ALU.add)
            LOOK = 2
            for (ss, sl) in seqs:
                keys = [(ss + c * 128) for c in range(sl // 128)] + [(TG + c * 128) for c in range(n_ctx // 128)]
                nk_ = len(keys)
                for q0 in range(ss, ss + sl, 512):
                    nq = min(512, ss + sl - q0)
                    BO, BD = acc_pairs[acc_rr[0] % 2]
                    acc_rr[0] += 1
                    sb_ = {}

                    def S(i):
                        b = bank()
                        kb = keys[i]
                        k.mm(b(0, nq), KhT[pp](kb, kb + 128), qn[pp](q0, q0 + nq), start=True, stop=False)
                        k.mm(b(0, nq), krT(kb, kb + 128), qr[pp](q0, q0 + nq), start=False, stop=True)
                        sb_[i] = b
                    for i in range(min(LOOK, nk_)):
                        S(i)
                    for i in range(nk_):
                        if i + LOOK < nk_:
                            S(i + LOOK)
                        kb = keys[i]
                        p_ = PT[pt_rr[0] % 4]
                        pt_rr[0] += 1
                        k.act(p_(0, nq), sb_.pop(i)(0, nq), AF.Exp, scale=scale)
                        k.mm(BO(0, nq), Vh[pp](kb, kb + 128), p_(0, nq), start=(i == 0), stop=(i == nk_ - 1))
                        k.mm(BD(0, nq), ones(), p_(0, nq), start=(i == 0), stop=(i == nk_ - 1))
                    rc = rec[acc_rr[0] % 2]
                    k.act(rc(0, nq), BD(0, nq), AF.Ln)
                    k.act(rc(0, nq), rc(0, nq), AF.Exp, scale=-1.0)
                    k.tt(oT(hd * TG + q0, hd * TG + q0 + nq), BO(0, nq), rc(0, nq), ALU.mult)
        bank_set[0] = [0, 1, 2, 3, 4, 5, 6]

        k.mark('g%d_B_done' % ci)
        SB.top = L1_TOP
        sq = [SB.alloc(BF16, 512, 'sq0'), SB.alloc(BF16, 512, 'sq1')]
        rbuf = SB.alloc(F32, 512, 'r')
        tbuf = [SB.alloc(F32, 512, 't0'), SB.alloc(F32, 512, 't1')]
        ysb = SB.alloc(F32, 8 * 512, 'ysb')
        for ti in (range(ntile) if 'c' in stages else ()):
            t0 = ti * 512
            proj_norm_res(2, lambda q: (wload([wpanel(w_o, q * 512, 512, 8)], key=('wo', q)), 512),
                          lambda kc: oT(kc * TG + t0, kc * TG + t0 + 512), 8,
                          lambda kc: xc(kc, t0, 512), 512, 1, 2, ci, ysb, sq, rbuf, tbuf)
        k.mark('g%d_C_done' % ci)
        if 'f1' in stages:
            ffn(1)
        k.mark('g%d_f1_done' % ci)
        for kc in range(8):
            k.dma('sp', yT[kc * 128:(kc + 1) * 128, g_off:g_off + TG], xc(kc, 0, TG).ap, R=[xc(kc, 0, TG)],
                  is_output=True)

    if 1 in groups:
        run_group(512, T_S, [(0, T_S)], 1, False)
    if 0 in groups:
        run_group(0, 512, [(0, 256), (256, 256)], 0, True)
    k.finish('sp')
    k.build()
    return nc, k


_CACHE = {}


def _rope_tables():
    half = 16
    inv = (10000.0 ** (-np.arange(half, dtype=np.float32) / half)).astype(np.float32)
    t = np.arange(T_S)
    r = (t // 64).astype(np.float32)
    col = (t % 64).astype(np.float32)
    cosT = np.zeros((ROPE, T_S), np.float32)
    sinT = np.zeros((ROPE, T_S), np.float32)
    for a, pos in enumerate((r, col)):
        ang = (pos[None, :] * inv[:, None]).astype(np.float32)
        c, s = np.cos(ang).astype(np.float32), np.sin(ang).astype(np.float32)
        cosT[a * 32:a * 32 + 16] = c
        cosT[a * 32 + 16:a * 32 + 32] = c
        sinT[a * 32:a * 32 + 16] = -s
        sinT[a * 32 + 16:a * 32 + 32] = s
    return cosT, sinT


def kernel(x_prompt, x_sample, cache_ckv, cache_krope, c, c_ctx, ada_w, ada_b, norm_g,
           gm_w_in, gm_b_in, gm_g_v, gm_w_s, gm_b_s, gm_w_out,
           mla_w_dq, mla_g_q, mla_w_uq, mla_w_dkv, mla_g_kv, mla_w_uk, mla_w_uv, mla_w_o,
           ffn_w_up, ffn_conv_w, ffn_conv_b, ffn_w_down):
    f = lambda a: np.ascontiguousarray(np.asarray(a, np.float32))
    if 'nc' not in _CACHE:
        _CACHE['nc'] = build_program()[0]
    nc = _CACHE['nc']
    cosT, sinT = _rope_tables()
    x_prompt, x_sample = f(x_prompt), f(x_sample)
    shared = {
        "bv_bc": f(np.broadcast_to(f(gm_b_in)[0, DI:], (128, DI))),
        "bs_bc": f(np.broadcast_to(f(gm_b_s)[0].reshape(-1), (128, NG * 128))),
        "wsT": f(np.transpose(f(gm_w_s)[0], (2, 0, 1)).reshape(128, NG * 128)),
        "cosT": cosT, "sinT": sinT,
        "ada_w": f(ada_w), "gm_w_in": f(gm_w_in)[0], "gm_w_out": f(gm_w_out)[0],
        "w_dq": f(mla_w_dq)[0], "w_uq": f(mla_w_uq)[0], "w_dkv": f(mla_w_dkv)[0],
        "w_uk": f(mla_w_uk)[0], "w_uv": f(mla_w_uv)[0], "w_o": f(mla_w_o)[0],
        "w_up": f(ffn_w_up), "w_down": f(ffn_w_down),
    }
    base = np.zeros((128, NPAR), np.float32)
    base[:, PM['ada_b']:PM['ada_b'] + 96] = fm(f(ada_b))
    base[:, PM['norm_g']:PM['norm_g'] + 64] = fm(f(norm_g))
    base[:, PM['b_in_u']:PM['b_in_u'] + 16] = fm(f(gm_b_in)[0, :DI])
    base[:, PM['g_q']:PM['g_q'] + 3] = fm(f(mla_g_q)[0])
    base[:, PM['g_v']:PM['g_v'] + 16] = fm(f(gm_g_v)[0])
    base[:, PM['g_kv']:PM['g_kv'] + 2] = fm(f(mla_g_kv)[0])
    base[:, PM['conv_w']:PM['conv_w'] + 264] = fm(f(ffn_conv_w))
    base[:, PM['conv_b']:PM['conv_b'] + 88] = fm(f(ffn_conv_b))
    in_maps = []
    for i in range(NCORES):
        xp = x_prompt[2 * i:2 * i + 2].reshape(512, D)
        xs = x_sample[i]
        xT = np.ascontiguousarray(np.concatenate([xp, xs], axis=0).T)
        p = base.copy()
        cond = np.stack([f(c_ctx), f(c)[i]], axis=-1)
        p[:, PM['cond']:PM['cond'] + 16] = cond.reshape(8, 128, 2).transpose(1, 0, 2).reshape(128, 16)
        m = dict(shared)
        m["xT"] = xT
        m["params"] = p
        m["cckvT"] = np.ascontiguousarray(f(cache_ckv)[i, 0].T)
        m["ckrT"] = np.ascontiguousarray(f(cache_krope)[i, 0].T)
        in_maps.append(m)
    res = run_bass_kernel_spmd(nc, in_maps, core_ids=list(range(NCORES)))
    y_prompt = np.zeros((16, SEQ_P, D), np.float32)
    y_sample = np.zeros((8, T_S, D), np.float32)
    new_ckv = np.zeros((16, 1, SEQ_P, KVR), np.float32)
    new_krope = np.zeros((16, 1, SEQ_P, ROPE), np.float32)
    for i in range(NCORES):
        r = res.results[i]
        y = np.asarray(r["yT"]).T
        y_prompt[2 * i:2 * i + 2] = y[:512].reshape(2, SEQ_P, D)
        y_sample[i] = y[512:]
        new_ckv[2 * i:2 * i + 2, 0] = np.asarray(r["nckvT"]).T.reshape(2, SEQ_P, KVR)
        new_krope[2 * i:2 * i + 2, 0] = np.asarray(r["nkrT"]).T.reshape(2, SEQ_P, ROPE)
    return (y_prompt, y_sample, new_ckv, new_krope)
```
